# Optimizing a Trainium2 kernel written in Bass

```python
import math
import jax, jax.numpy as jnp
from jax import lax
import numpy as np

D_MODEL = 1024
BATCH = 16
SEQ = 4096
DEPTH = 1

N_META = 16
E_CONV = D_MODEL
K_CONV_A = 3
E_LRU = D_MODEL
LRU_HEAD_DIM = 256
N_LRU_HEADS = E_LRU // LRU_HEAD_DIM
K_CONV_B = 4
C_LRU = 8.0
RMS_EPS = 1e-6
N_IN = 4 * E_CONV + 2 * E_LRU + 2 * D_MODEL

kernel_name = "hybrid_shortconv_rglru_gated_merge"


def rms_norm(x, g):
    xf = x.astype(jnp.float32)
    y = xf * lax.rsqrt(jnp.mean(xf * xf, axis=-1, keepdims=True) + RMS_EPS)
    return (y * g.astype(jnp.float32)).astype(x.dtype)


def causal_dwconv(x, w):
    k_len = w.shape[0]
    s = x.shape[1]
    xp = jnp.pad(x, ((0, 0), (k_len - 1, 0), (0, 0)))
    y = xp[:, 0:s] * w[0]
    for k in range(1, k_len):
        y = y + xp[:, k:k + s] * w[k]
    return y


def rg_lru(x, w_a, b_a, w_i, b_i, lam):
    bn, s, e = x.shape
    xf = x.astype(jnp.float32)
    xh = xf.reshape(bn, s, N_LRU_HEADS, LRU_HEAD_DIM)
    r = jax.nn.sigmoid(jnp.einsum('bshi,hij->bshj', xh, w_a.astype(jnp.float32)) + b_a.astype(jnp.float32)).reshape(bn, s, e)
    i = jax.nn.sigmoid(jnp.einsum('bshi,hij->bshj', xh, w_i.astype(jnp.float32)) + b_i.astype(jnp.float32)).reshape(bn, s, e)
    log_a = -C_LRU * r * jax.nn.softplus(-lam.astype(jnp.float32))
    a = jnp.exp(log_a)
    mult = jnp.sqrt(-jnp.expm1(2.0 * log_a))
    is_first = (jnp.arange(s) == 0)[None, :, None]
    mult = jnp.where(is_first, 1.0, mult)
    u = xf * i * mult

    def combine(left, right):
        a_l, b_l = left
        a_r, b_r = right
        return a_l * a_r, a_r * b_l + b_r

    _, h = lax.associative_scan(combine, (a, u), axis=1)
    return h.astype(x.dtype)


def setup_inputs(seed: int = 0) -> dict:
    key = jax.random.key(seed)
    ks = jax.random.split(key, 18)
    f32 = jnp.float32
    nrm = lambda k, shape, scale: jax.random.normal(k, shape, f32) * scale
    x = jax.random.normal(ks[0], (BATCH, SEQ, D_MODEL), f32)
    meta = nrm(ks[1], (N_META, D_MODEL), 1.0)
    norm_g = 1.0 + nrm(ks[2], (DEPTH, D_MODEL), 0.02)
    w_in = nrm(ks[3], (DEPTH, D_MODEL, N_IN), D_MODEL ** -0.5)
    b_gate = nrm(ks[4], (DEPTH, 2 * D_MODEL), 0.02)
    conv_a_w = nrm(ks[5], (DEPTH, K_CONV_A, E_CONV), K_CONV_A ** -0.5)
    w_proj_a = nrm(ks[6], (DEPTH, E_CONV, D_MODEL), E_CONV ** -0.5)
    conv_b_w = nrm(ks[7], (DEPTH, K_CONV_B, E_LRU), K_CONV_B ** -0.5)
    conv_b_b = nrm(ks[8], (DEPTH, E_LRU), 0.02)
    w_rg_a = nrm(ks[9], (DEPTH, N_LRU_HEADS, LRU_HEAD_DIM, LRU_HEAD_DIM), LRU_HEAD_DIM ** -0.5)
    b_rg_a = nrm(ks[10], (DEPTH, N_LRU_HEADS, LRU_HEAD_DIM), 0.02)
    w_rg_i = nrm(ks[11], (DEPTH, N_LRU_HEADS, LRU_HEAD_DIM, LRU_HEAD_DIM), LRU_HEAD_DIM ** -0.5)
    b_rg_i = nrm(ks[12], (DEPTH, N_LRU_HEADS, LRU_HEAD_DIM), 0.02)
    a_c = jax.random.uniform(ks[13], (DEPTH, E_LRU), f32, 0.9, 0.999)
    s_base = a_c ** (1.0 / C_LRU)
    lru_param = jnp.log(s_base) - jnp.log1p(-s_base)
    w_proj_b = nrm(ks[14], (DEPTH, E_LRU, D_MODEL), E_LRU ** -0.5)
    w_out = nrm(ks[15], (DEPTH, D_MODEL, D_MODEL), D_MODEL ** -0.5)
    final_norm_g = 1.0 + nrm(ks[16], (D_MODEL,), 0.02)
    return {"x": x, "meta": meta, "norm_g": norm_g, "w_in": w_in, "b_gate": b_gate,
            "conv_a_w": conv_a_w, "w_proj_a": w_proj_a, "conv_b_w": conv_b_w, "conv_b_b": conv_b_b,
            "w_rg_a": w_rg_a, "b_rg_a": b_rg_a, "w_rg_i": w_rg_i, "b_rg_i": b_rg_i,
            "lru_param": lru_param, "w_proj_b": w_proj_b, "w_out": w_out, "final_norm_g": final_norm_g}


def reference(x, meta, norm_g, w_in, b_gate, conv_a_w, w_proj_a, conv_b_w, conv_b_b,
              w_rg_a, b_rg_a, w_rg_i, b_rg_i, lru_param, w_proj_b, w_out, final_norm_g):
    bn = x.shape[0]
    meta_b = jnp.broadcast_to(meta[None].astype(x.dtype), (bn, N_META, D_MODEL))
    h = jnp.concatenate([meta_b, x], axis=1)
    split_idx = [E_CONV, 2 * E_CONV, 3 * E_CONV, 4 * E_CONV,
                 4 * E_CONV + E_LRU, 4 * E_CONV + 2 * E_LRU, 4 * E_CONV + 2 * E_LRU + D_MODEL]
    for l in range(DEPTH):
        xn = rms_norm(h, norm_g[l])
        proj = xn @ w_in[l]
        b_a, c_a, h_a, z_a, x_b, z_b, g_a, g_b = jnp.split(proj, split_idx, axis=-1)
        y_a = b_a * causal_dwconv(c_a * h_a, conv_a_w[l])
        y_a = (y_a * jax.nn.silu(z_a)) @ w_proj_a[l]
        x_c = causal_dwconv(x_b, conv_b_w[l]) + conv_b_b[l]
        y_b = rg_lru(x_c, w_rg_a[l], b_rg_a[l], w_rg_i[l], b_rg_i[l], lru_param[l])
        y_b = (y_b * jax.nn.silu(z_b)) @ w_proj_b[l]
        gate_a = jax.nn.sigmoid(g_a + b_gate[l, :D_MODEL])
        gate_b = jax.nn.sigmoid(g_b + b_gate[l, D_MODEL:])
        merged = gate_a * y_a + gate_b * y_b
        h = h + merged @ w_out[l]
    out = rms_norm(h, final_norm_g)
    return out[:, N_META:]
```

```python
import contextlib
import numpy as np
import concourse.bass as bass
import concourse.mybir as mybir
from concourse.bass_utils import run_bass_kernel_spmd

F32 = mybir.dt.float32
BF16 = mybir.dt.bfloat16
AF = mybir.ActivationFunctionType
ALU = mybir.AluOpType

D = 1024
KC = 8
SW = 512
NMETA = 16
EPS = 1e-6
N_CORES = 8


class _Op:
    __slots__ = ("eng", "fn", "pos", "dma", "dval", "waits", "targeted", "seq")


class Prog:
    ENGS = ("pe", "act", "dve", "pool", "sp")

    def __init__(self):
        self.by_eng = {e: [] for e in self.ENGS}
        self.last_w = {}
        self.readers = {}
        self.dma_cnt = {}
        self.waited_c = {}
        self.waited_d = {}

    def op(self, eng, fn, reads=(), writes=(), dma=None):
        o = _Op()
        o.eng, o.fn, o.dma = eng, fn, dma
        o.targeted, o.seq, o.dval = False, None, None
        lst = self.by_eng[eng]
        o.pos = len(lst)
        deps = {}
        for k in reads:
            w = self.last_w.get(k)
            if w is not None:
                deps[id(w)] = (w, True)
        for k in writes:
            w = self.last_w.get(k)
            if w is not None and id(w) not in deps:
                deps[id(w)] = (w, False)
            for r in self.readers.get(k, ()):
                if id(r) not in deps:
                    deps[id(r)] = (r, False)
        waits = []
        dmax = {}
        for d, raw in deps.values():
            if d.dma is not None and (d.dma not in dmax or dmax[d.dma].dval < d.dval):
                dmax[d.dma] = d
        for d in dmax.values():
            key = (eng, d.dma)
            if self.waited_d.get(key, 0) >= d.dval:
                continue
            self.waited_d[key] = d.dval
            waits.append(d)
        for d, raw in deps.values():
            if d is o or d.dma is not None:
                continue
            if d.eng == eng and dma is None:
                if eng == "pe" or not raw:
                    continue
            key = (eng, d.eng)
            if self.waited_c.get(key, -1) >= d.pos:
                continue
            self.waited_c[key] = d.pos
            d.targeted = True
            waits.append(d)
        o.waits = waits
        if dma is not None:
            self.dma_cnt[dma] = self.dma_cnt.get(dma, 0) + 16
            o.dval = self.dma_cnt[dma]
        for k in reads:
            self.readers.setdefault(k, []).append(o)
        for k in writes:
            self.last_w[k] = o
            self.readers[k] = []
        lst.append(o)
        return o

    def finalize(self):
        for e in self.ENGS:
            n = 0
            for o in self.by_eng[e]:
                if o.dma is None and o.targeted:
                    n += 1
                    o.seq = n

    def emit(self, eng, handle, csem, dsem):
        for o in self.by_eng[eng]:
            for d in o.waits:
                if d.dma is not None:
                    handle.wait_ge(dsem[d.dma], d.dval)
                else:
                    handle.wait_ge(csem[d.eng], d.seq)
            if o.fn is None:
                continue
            ins = o.fn(handle)
            if o.dma is not None:
                ins.then_inc(dsem[o.dma], 16)
            elif o.targeted:
                ins.then_inc(csem[o.eng], 1)


CAW, CBW, CBB, BRA, BRI, LAM, BG = 0, 24, 56, 64, 72, 80, 88
NV = 104
D_CAWH, D_HBRA, D_HBRI, D_HBG, D_C2, D_C4, D_TMP, D_ZERO = 0, 24, 32, 40, 56, 64, 72, 88
NDV = 92

NSLOT = 8
NXT = 5


def build_nc(NSEQ, SEQ, T):
    assert SEQ % T == 0 and T % SW == 0
    NS = T // SW
    NG = T // 128
    nc = bass.Bass("TRN2", target_bir_lowering=False)
    din = lambda name, shape: nc.dram_tensor(name, shape, F32, kind="ExternalInput").ap()
    x_d = din("x", [NSEQ, SEQ, D])
    meta_d = din("meta", [NMETA, D])
    norm_g_d = din("norm_g", [1, D])
    w_in_d = din("w_in", [D, 8 * D])
    b_gate_d = din("b_gate", [2, D])
    conv_a_w_d = din("conv_a_w", [3, D])
    w_proj_a_d = din("w_proj_a", [D, D])
    conv_b_w_d = din("conv_b_w", [4, D])
    conv_b_b_d = din("conv_b_b", [1, D])
    w_rg_a_d = din("w_rg_a", [4, 256, 256])
    b_rg_a_d = din("b_rg_a", [1, D])
    w_rg_i_d = din("w_rg_i", [4, 256, 256])
    b_rg_i_d = din("b_rg_i", [1, D])
    lru_d = din("lru_param", [1, D])
    w_proj_b_d = din("w_proj_b", [D, D])
    w_out_d = din("w_out", [D, D])
    fg_d = din("final_norm_g", [1, D])
    out_d = nc.dram_tensor("out", [NSEQ, SEQ, D], F32, kind="ExternalOutput").ap()
    wsc = nc.dram_tensor("wsc", [80, 128, D], BF16, kind="Internal").ap()

    P = Prog()
    es = contextlib.ExitStack()
    with es:
        sb = lambda name, shape, dt: es.enter_context(nc.sbuf_tensor(name, shape, dt))
        xnT = sb("xnT", [128, KC, T + NMETA], BF16)
        yaT = sb("yaT", [128, KC, T], BF16)
        ybT = sb("ybT", [128, KC, T], BF16)
        mT = sb("mT", [128, KC, T], BF16)
        wo = sb("wo", [128, KC, D], BF16)
        wrg = sb("wrg", [128, 2, 8, 256], BF16)
        gbc = sb("gbc", [128, D], F32)
        fgbc = sb("fgbc", [128, D], F32)
        slots = [sb(f"slot{i}", [128, KC, 128], BF16) for i in range(NSLOT)]
        xt = [sb(f"xt{i}", [128, D], F32) for i in range(NXT)]
        xs = [sb(f"xs{i}", [128, D], BF16) for i in range(3)]
        junk = sb("junk", [128, D], BF16)
        xcb = [sb(f"xcb{i}", [128, 2, SW], BF16) for i in range(2)]
        vecs = sb("vecs", [NV, 128], F32)
        vT = sb("vT", [128, NV], F32)
        dv = sb("dv", [128, NDV], F32)
        identf = sb("identf", [128, 128], F32)
        identb = sb("identb", [128, 128], BF16)
        mhalf = sb("mhalf", [128, 1], F32)
        stA = sb("stA", [128, KC, 2], F32)
        stB = sb("stB", [128, KC, 3], F32)
        stH = sb("stH", [128, KC, 1], F32)
        mtA = sb("mtA", [128, KC, 2], F32)
        mtB = sb("mtB", [128, KC, 3], F32)
        mtH = sb("mtH", [128, KC, 1], F32)
        sq = [sb(f"sq{i}", [128, 4], F32) for i in range(4)]
        SCW = SW + 4
        scr = {}

        def scratch(name, n):
            scr[name] = [sb(f"{name}{i}", [128, SCW], F32) for i in range(n)]

        scratch("xbuf", 4)
        scratch("xc", 4)
        scratch("tr", 2)
        scratch("ti", 2)
        scratch("aa", 2)
        scratch("mm", 4)
        scratch("tz", 4)
        scratch("tza", 2)
        scratch("ch", 2)
        scratch("cv", 2)
        scratch("atmp", 1)
        for nm in ("xbuf", "xc", "mm", "tz"):
            scr[nm] += [sb(f"{nm}M{i}", [128, NMETA + 4], F32) for i in range(2)]
        xcb.append(sb("xcbM", [128, 2, NMETA], BF16))
        ps = [es.enter_context(nc.psum_tensor(f"ps{i}", [128, SW], F32)) for i in range(8)]

        cnt = {"ps": 0, "slot": 0, "sq": 0, "xs": 0}

        def psum():
            i = cnt["ps"] % 8
            cnt["ps"] += 1
            return ps[i], ("ps", i)

        def dma(out, in_, reads, writes, key):
            P.op("sp", lambda e: e.dma_start(out=out, in_=in_), reads, writes, dma=key)

        def act(out, in_, func, reads, writes, bias=None, scale=None, accum_out=None):
            kw = {}
            if bias is not None:
                kw["bias"] = bias
            if scale is not None:
                kw["scale"] = scale
            if accum_out is not None:
                kw["accum_out"] = accum_out
            P.op("act", lambda e: e.activation(out=out, in_=in_, func=func, **kw), reads, writes)

        def tt(out, in0, in1, op, reads, writes, eng="dve"):
            P.op(eng, lambda e: e.tensor_tensor(out=out, in0=in0, in1=in1, op=op), reads, writes)

        def ts(out, in0, s1, s2, op0, op1, reads, writes, eng="dve"):
            if op1 is None:
                P.op(eng, lambda e: e.tensor_scalar(out=out, in0=in0, scalar1=s1, scalar2=None, op0=op0), reads, writes)
            else:
                P.op(eng, lambda e: e.tensor_scalar(out=out, in0=in0, scalar1=s1, scalar2=s2, op0=op0, op1=op1), reads, writes)

        def stt(out, in0, scalar, in1, op0, op1, reads, writes):
            P.op("dve", lambda e: e.scalar_tensor_tensor(out=out, in0=in0, scalar=scalar, in1=in1, op0=op0, op1=op1), reads, writes)

        def cp(out, in_, reads, writes, eng="dve"):
            P.op(eng, lambda e: e.tensor_copy(out=out, in_=in_), reads, writes)

        def mm(out, lhsT, rhs, start, stop, reads, writes):
            P.op("pe", lambda e: e.matmul(out, lhsT=lhsT, rhs=rhs, start=start, stop=stop), reads, writes)

        def tr_(out, in_, ident, reads, writes):
            P.op("pe", lambda e: e.transpose(out=out, in_=in_, identity=ident), reads, writes)

        ya_keys = [("ya", j, s) for j in range(KC) for s in range(NS)]
        yb_keys = [("yb", j, s) for j in range(KC) for s in range(NS)]
        m_keys = [("m", j, s) for j in range(KC) for s in range(NS)]

        P.op("pool", lambda e: e.memset(identf[:], 0.0), writes=["identf"])
        P.op("pool", lambda e: e.affine_select(out=identf[:], in_=identf[:], compare_op=ALU.not_equal, fill=1.0, base=0,
                                               pattern=[[-1, 128]], channel_multiplier=1), reads=["identf"], writes=["identf"])
        P.op("pool", lambda e: e.memset(mhalf[:], -0.5), writes=["mhalf"])
        kA = [("stA", c) for c in range(KC)]
        kB = [("stB", c) for c in range(KC)]
        kH = [("stH", c) for c in range(KC)]
        P.op("pool", lambda e: e.memset(stA[:], 0.0), writes=kA)
        P.op("pool", lambda e: e.memset(stB[:], 0.0), writes=kB)
        P.op("pool", lambda e: e.memset(stH[:], 0.0), writes=kH)
        cp(identb[:], identf[:], ["identf"], ["identb"])
        P.op("pool", lambda e: e.memset(dv[:, D_ZERO:D_ZERO + 1], 0.0), writes=["dvz"])
        vec_srcs = [(conv_a_w_d, CAW, 24), (conv_b_w_d, CBW, 32), (conv_b_b_d, CBB, 8), (b_rg_a_d, BRA, 8),
                    (b_rg_i_d, BRI, 8), (lru_d, LAM, 8), (b_gate_d, BG, 16)]
        for i, (src, r0, n) in enumerate(vec_srcs):
            dma(vecs[r0:r0 + n, :], src.rearrange("k (j p) -> (k j) p", p=128), [], [("vecs", i)], "cst")
        dma(gbc[:], norm_g_d.partition_broadcast(128), [], ["gbc"], "cstg")
        dma(fgbc[:], fg_d.partition_broadcast(128), [], ["fgbc"], "cstf")
        pst, pk = psum()
        tr_(pst[:, 0:NV], vecs[:], identf[0:NV, 0:NV], [("vecs", i) for i in range(7)] + ["identf"], [pk])
        cp(vT[:], pst[:, 0:NV], [pk], ["vT"])
        ts(dv[:, D_CAWH:D_CAWH + 24], vT[:, CAW:CAW + 24], 0.5, None, ALU.mult, None, ["vT"], ["dv"])
        ts(dv[:, D_HBRA:D_HBRA + 16], vT[:, BRA:BRA + 16], 0.5, None, ALU.mult, None, ["vT"], ["dv"])
        ts(dv[:, D_HBG:D_HBG + 16], vT[:, BG:BG + 16], 0.5, None, ALU.mult, None, ["vT"], ["dv"])
        act(dv[:, D_TMP:D_TMP + 8], vT[:, LAM:LAM + 8], AF.Exp, ["vT"], ["dv"], scale=-1.0)
        act(dv[:, D_TMP + 8:D_TMP + 16], dv[:, D_TMP:D_TMP + 8], AF.Ln, ["dv"], ["dv"], bias=1.0, scale=1.0)
        ts(dv[:, D_C2:D_C2 + 8], dv[:, D_TMP + 8:D_TMP + 16], -4.0, None, ALU.mult, None, ["dv"], ["dv"])
        ts(dv[:, D_C4:D_C4 + 8], dv[:, D_TMP + 8:D_TMP + 16], -8.0, None, ALU.mult, None, ["dv"], ["dv"])

        xtc = {"n": 0}

        def rstd_act(sqt, n, sqk):
            act(sqt[0:n, 1:2], sqt[0:n, 0:1], AF.Sqrt, [sqk], [sqk], bias=EPS, scale=1.0 / D)

        def rstd_dve(sqt, n, sqk):
            P.op("dve", lambda e: e.reciprocal(out=sqt[0:n, 2:3], in_=sqt[0:n, 1:2]), [sqk], [sqk])

        def item_load(it):
            bi = xtc["n"] % NXT
            xtc["n"] += 1
            it["xb"] = bi
            n = it["n"]
            dma(xt[bi][0:n, :], it["src"], [], [("xt", bi)], f"xl{bi}")

        def new_sq(it):
            sqi = cnt["sq"] % 4
            cnt["sq"] += 1
            it["sq"] = (sq[sqi], ("sq", sqi))

        def in_A_act(it):
            n, bi = it["n"], it["xb"]
            new_sq(it)
            sqt, sqk = it["sq"]
            act(junk[0:n, :], xt[bi][0:n, :], AF.Square, [("xt", bi)], [sqk], accum_out=sqt[0:n, 0:1])
            rstd_act(sqt, n, sqk)

        def in_A_pool(it):
            sqt, sqk = it["sq"]
            rstd_dve(sqt, it["n"], sqk)

        def in_B1_dve(it):
            n, bi = it["n"], it["xb"]
            sqt, sqk = it["sq"]
            ts(xt[bi][0:n, :], xt[bi][0:n, :], sqt[0:n, 2:3], None, ALU.mult, None, [("xt", bi), sqk], [("xt", bi)])

        def in_B1_pool(it):
            n, bi = it["n"], it["xb"]
            jb = cnt["xs"] % 3
            cnt["xs"] += 1
            xsb, xsk = xs[jb], ("xs", jb)
            tt(xsb[0:n, :], xt[bi][0:n, :], gbc[0:n, :], ALU.mult, [("xt", bi), "gbc"], [xsk], eng="pool")
            it["xs"] = (xsb, xsk)

        def in_B2(it):
            n = it["n"]
            xsb, xsk = it["xs"]
            pt, pk = psum()
            ptb = pt[:].bitcast(BF16)
            for kc in range(KC):
                tr_(ptb[:, kc * 128:kc * 128 + n], xsb[0:n, kc * 128:(kc + 1) * 128], identb[0:n, 0:n], [xsk, "identb"], [pk])
            it["pt"] = (ptb, pk)

        def in_C(it):
            n, g = it["n"], it["g"]
            ptb, pk = it["pt"]
            c0 = it.get("c0", g * 128)
            xk = it.get("xk", ("xn", (g * 128) // SW))
            act(xnT[:, :, c0:c0 + n], ptb.rearrange("p (k t) -> p k t", k=KC)[:, :, 0:n], AF.Copy, [pk], [xk])

        def p3_A_pe(it):
            g = it["g"]
            s_ = (g * 128) // SW
            it["po"] = []
            for half in range(2):
                po, pok = psum()
                for kc in range(KC):
                    mm(po[:], mT[:, kc, g * 128:(g + 1) * 128], wo[:, kc, half * SW:(half + 1) * SW], kc == 0, kc == KC - 1,
                       [("m", kc, s_), "wo"], [pok])
                it["po"].append((po, pok))

        def p3_A_dve(it, half):
            bi = it["xb"]
            xtb, xtk = xt[bi], ("xt", bi)
            if True:
                po, pok = it["po"][half]
                stt(xtb[:, half * SW:(half + 1) * SW], po[:], 0.5, xtb[:, half * SW:(half + 1) * SW], ALU.mult, ALU.add,
                    [pok, xtk], [xtk])

        def p3_B_act(it):
            bi = it["xb"]
            new_sq(it)
            sqt, sqk = it["sq"]
            act(junk[:], xt[bi][:], AF.Square, [("xt", bi)], [sqk], accum_out=sqt[:, 0:1])
            rstd_act(sqt, 128, sqk)

        def p3_B_pool(it):
            sqt, sqk = it["sq"]
            rstd_dve(sqt, 128, sqk)

        def p3_C(it):
            bi = it["xb"]
            xtb, xtk = xt[bi], ("xt", bi)
            sqt, sqk = it["sq"]
            stt(xtb[:], xtb[:], sqt[:, 2:3], fgbc[:], ALU.mult, ALU.mult, [xtk, sqk, "fgbc"], [xtk])
            dma(it["dst"], xtb[:], [xtk], [], f"st{bi}")

        def run_boundary(p3, nin, tail_hook=None):
            n = max(len(p3), len(nin))
            loads = []
            for i in range(n):
                if i < len(p3):
                    loads.append(p3[i])
                if i < len(nin):
                    loads.append(nin[i])
            nl = [0]
            for it in loads:
                it["fin"] = False

            def load_more(maxahead):
                while nl[0] < len(loads) and nl[0] < maxahead:
                    k = nl[0]
                    if k >= NXT and not loads[k - NXT]["fin"]:
                        break
                    item_load(loads[k])
                    nl[0] += 1
            per = (1 if p3 else 0) + (1 if nin else 0)
            P_ = lambda k: p3[k] if 0 <= k < len(p3) else None
            I_ = lambda k: nin[k] if 0 <= k < len(nin) else None
            for t in range(n + 3):
                if t == n and tail_hook is not None:
                    tail_hook()
                load_more(per * (t + 2))
                if I_(t - 2):
                    in_B2(I_(t - 2))
                if P_(t):
                    p3_A_pe(P_(t))
                if P_(t - 1):
                    p3_B_act(P_(t - 1))
                if I_(t):
                    in_A_act(I_(t))
                if I_(t - 2):
                    in_C(I_(t - 2))
                if I_(t - 1):
                    in_B1_dve(I_(t - 1))
                    in_B1_pool(I_(t - 1))
                    I_(t - 1)["fin"] = True
                if P_(t - 1):
                    p3_B_pool(P_(t - 1))
                load_more(per * (t + 2) + 1)
                if P_(t):
                    p3_A_dve(P_(t), 0)
                if P_(t - 1):
                    p3_C(P_(t - 1))
                    P_(t - 1)["fin"] = True
                if I_(t):
                    in_A_pool(I_(t))
                if P_(t):
                    p3_A_dve(P_(t), 1)

        def in_items(b, n):
            return [dict(kind="IN", g=g, n=128, src=x_d[b, n * T + g * 128:n * T + (g + 1) * 128, :]) for g in range(NG)]

        def p3_items(b, n):
            return [dict(kind="P3", g=g, n=128, src=x_d[b, n * T + g * 128:n * T + (g + 1) * 128, :],
                         dst=out_d[b, n * T + g * 128:n * T + (g + 1) * 128, :]) for g in range(NG)]

        stg = [yaT[:].bitcast(F32), ybT[:].bitcast(F32), xnT[:].bitcast(F32)[:, :, 0:T // 2]]
        stg_keys = [ya_keys, yb_keys, [("xn", s) for s in range(NS)] + [("xn", "m")]]
        NSTG = 3
        mflat = mT[:].rearrange("p k t -> p (k t)")
        wb = [mflat[:, i * 4096:(i + 1) * 4096].rearrange("p (j k c) -> p j k c", j=4, k=KC) for i in range(2)]
        wb_keys = [[("m", j, s) for j in range(4 * i, 4 * i + 4) for s in range(NS)] for i in range(2)]
        def prologue_weights():
            rounds = []
            for g in (4, 5, 0, 1, 2, 3, 6, 7):
                for jh in range(2):
                    c0 = g * D + jh * 512
                    rounds.append(dict(src=w_in_d[:, c0:c0 + 512].rearrange("(kc p) n -> p kc n", p=128), blk0=g * 8 + jh * 4))
            for jh in range(2):
                rounds.append(dict(src=w_proj_a_d[:, jh * 512:(jh + 1) * 512].rearrange("(kc p) n -> p kc n", p=128), blk0=64 + jh * 4))
            for jh in range(2):
                rounds.append(dict(src=w_proj_b_d[:, jh * 512:(jh + 1) * 512].rearrange("(kc p) n -> p kc n", p=128), blk0=72 + jh * 4))
            for jh in range(2):
                rounds.append(dict(src=w_out_d[:, jh * 512:(jh + 1) * 512].rearrange("(kc p) n -> p kc n", p=128),
                                   sb=wo[:, :, jh * 512:(jh + 1) * 512], sbk=["wo"]))
            for gi, wsrc in enumerate((w_rg_a_d, w_rg_i_d)):
                rounds.append(dict(src=wsrc.rearrange("h (kc p) n -> p (h kc) n", p=128), sb=wrg[:, gi, :, :], sbk=["wrg"], ncol=256))

            def do_load(r):
                rd = rounds[r]
                i = r % NSTG
                dst = stg[i] if "ncol" not in rd else stg[i][:, :, 0:rd["ncol"]]
                dma(dst, rd["src"], [], stg_keys[i], f"pl{i}")

            nwb = [0]

            def do_cast_store(r):
                rd = rounds[r]
                i = r % NSTG
                if "blk0" in rd:
                    wi = nwb[0] % 2
                    nwb[0] += 1
                    o_ap = wb[wi].rearrange("p j k c -> p k j c")
                    i_ap = stg[i].rearrange("p k (j c) -> p k j c", j=4)
                    cp(o_ap, i_ap, stg_keys[i], wb_keys[wi])
                    b0 = rd["blk0"]
                    dma(wsc[b0:b0 + 4].rearrange("b p n -> p b n"), wb[wi].rearrange("p j k c -> p j (k c)"),
                        wb_keys[wi], [("wsc", b0 + q) for q in range(4)], f"pst{wi}")
                else:
                    src = stg[i] if "ncol" not in rd else stg[i][:, :, 0:rd["ncol"]]
                    act(rd["sb"], src, AF.Copy, stg_keys[i], rd["sbk"])

            for r in range(min(NSTG - 1, len(rounds))):
                do_load(r)
            for r in range(len(rounds)):
                if r + NSTG - 1 < len(rounds):
                    do_load(r + NSTG - 1)
                do_cast_store(r)

        tiles = [("seq", b, n) for b in range(NSEQ) for n in range(SEQ // T)]
        seq_blocks = []
        for t in tiles:
            for h in range(4):
                for g in (4, 5):
                    for c in (2 * h, 2 * h + 1):
                        seq_blocks.append(g * 8 + c)
                for c in (2 * h, 2 * h + 1):
                    for g in (0, 1, 2, 3):
                        seq_blocks.append(g * 8 + c)
            if t[0] != "meta":
                for j in range(KC):
                    seq_blocks += [48 + j, 56 + j, 64 + j, 72 + j]
        bs = {"issued": 0, "used": 0, "done": 0}

        def issue_loads():
            upto = min(bs["done"] + NSLOT, len(seq_blocks))
            while bs["issued"] < upto:
                k = bs["issued"]
                blk = seq_blocks[k]
                si = k % NSLOT
                dma(slots[si][:].rearrange("p k c -> p (k c)"), wsc[blk], [("wsc", blk)], [("slot", si)], f"wl{si}")
                bs["issued"] += 1

        def next_block(blk):
            k = bs["used"]
            assert seq_blocks[k] == blk, (k, seq_blocks[k], blk)
            assert k < bs["issued"], (k, bs["issued"])
            bs["used"] += 1
            si = k % NSLOT
            return slots[si], ("slot", si)

        def blocks_done(n):
            bs["done"] += n
            assert bs["done"] == bs["used"]
            issue_loads()

        def proj_group(slot, skey, s, c0, w):
            pt, pk = psum()
            for kc in range(KC):
                mm(pt[:, 0:w], slot[:, kc, :], xnT[:, kc, c0:c0 + w], kc == 0, kc == KC - 1, [skey, ("xn", s)], [pk])
            return pt, pk

        unit = [0]
        zcol = dv[:, D_ZERO:D_ZERO + 1]

        def interleave(*lists):
            out = []
            n = max(len(l) for l in lists)
            for i in range(n):
                for l in lists:
                    if i < len(l):
                        out.append(l[i])
            for f in out:
                f()

        def pre_project(subtiles):
            cs = (0, 1)
            sl_xb = [next_block(4 * 8 + c) for c in cs]
            sl_zb = [next_block(5 * 8 + c) for c in cs]
            s, c0, w = subtiles[0]
            pz = [proj_group(sl_zb[ci][0], sl_zb[ci][1], s, c0, w) for ci in range(2)]
            pt = [proj_group(sl_xb[ci][0], sl_xb[ci][1], s, c0, w) for ci in range(2)]
            return dict(sl_xb=sl_xb, sl_zb=sl_zb, pz=pz, pt=pt, s=s)

        def phase1(subtiles, meta_sub=None, pre=None):
            for h in range(4):
                cs = (2 * h, 2 * h + 1)
                if h == 0 and pre:
                    sl_xb, sl_zb = pre["sl_xb"], pre["sl_zb"]
                else:
                    sl_xb = [next_block(4 * 8 + c) for c in cs]
                    sl_zb = [next_block(5 * 8 + c) for c in cs]
                bst = []

                def b_s123(s, c0, w, is_meta):
                    if is_meta:
                        u = 2
                    else:
                        u = unit[0] % 2
                        unit[0] += 1
                    st = dict(u=u, s=s, c0=c0, w=w, meta=is_meta)
                    xcbt, xcbk = xcb[u], ("xcb", u)
                    for ci, c in enumerate(cs):
                        if is_meta:
                            break
                        bi = u * 2 + ci
                        if h == 0 and pre and pre["s"] == s and not is_meta:
                            pz, pzk = pre["pz"][ci]
                        else:
                            pz, pzk = proj_group(sl_zb[ci][0], sl_zb[ci][1], s, c0, w)
                        act(scr["tz"][bi][:, 0:w], pz[:, 0:w], AF.Tanh, [pzk], [("tz", bi)], scale=0.5)
                        stt(scr["tz"][bi][:, 0:w], scr["tz"][bi][:, 0:w], 1.0, pz[:, 0:w], ALU.add, ALU.mult,
                            [("tz", bi), pzk], [("tz", bi)])
                    for ci, c in enumerate(cs):
                        bi = u * 2 + ci
                        xb_t, xb_k = scr["xbuf"][bi], ("xbuf", bi)
                        xc_t, xc_k = scr["xc"][bi], ("xc", bi)
                        if h == 0 and pre and pre["s"] == s and not is_meta:
                            pt, pk = pre["pt"][ci]
                        else:
                            pt, pk = proj_group(sl_xb[ci][0], sl_xb[ci][1], s, c0, w)
                        act(xb_t[:, 0:3], stB[:, c, :], AF.Copy, [("stB", c)], [xb_k])
                        act(xb_t[:, 3:3 + w], pt[:, 0:w], AF.Copy, [pk], [xb_k])
                        act(stB[:, c, :], xb_t[:, w:w + 3], AF.Copy, [xb_k], [("stB", c)])
                        ts(xc_t[:, 0:w], xb_t[:, 0:w], vT[:, CBW + c:CBW + c + 1], vT[:, CBB + c:CBB + c + 1],
                           ALU.mult, ALU.add, [xb_k, "vT"], [xc_k])
                        for k in range(1, 4):
                            stt(xc_t[:, 0:w], xb_t[:, k:k + w], vT[:, CBW + 8 * k + c:CBW + 8 * k + c + 1], xc_t[:, 0:w],
                                ALU.mult, ALU.add, [xb_k, xc_k, "vT"], [xc_k])
                        cp(xcbt[:, ci, 0:w], xc_t[:, 0:w], [xc_k], [xcbk])
                    return st

                def b_gates_pe(st):
                    u, s, c0, w = st["u"], st["s"], st["c0"], st["w"]
                    xcbt, xcbk = xcb[u], ("xcb", u)
                    gp = []
                    for ci, c in enumerate(cs):
                        pr, prk = psum()
                        for kc in range(2):
                            mm(pr[:, 0:w], wrg[:, 0, h * 2 + kc, ci * 128:(ci + 1) * 128], xcbt[:, kc, 0:w], kc == 0, kc == 1,
                               ["wrg", xcbk], [prk])
                        pi, pik = psum()
                        for kc in range(2):
                            mm(pi[:, 0:w], wrg[:, 1, h * 2 + kc, ci * 128:(ci + 1) * 128], xcbt[:, kc, 0:w], kc == 0, kc == 1,
                               ["wrg", xcbk], [pik])
                        gp.append((pr, prk, pi, pik))
                    st["gp"] = gp

                def b_gates_act(st, ci):
                    u, s, c0, w = st["u"], st["s"], st["c0"], st["w"]
                    c = cs[ci]
                    bi = u * 2 + ci
                    pr, prk, pi, pik = st["gp"][ci]
                    act(scr["tr"][ci][:, 0:w], pr[:, 0:w], AF.Tanh, [prk, "dv"], [("tr", ci)],
                        bias=dv[:, D_HBRA + c:D_HBRA + c + 1], scale=0.5)
                    act(scr["ti"][ci][:, 0:w], pi[:, 0:w], AF.Tanh, [pik, "dv"], [("ti", ci)],
                        bias=dv[:, D_HBRI + c:D_HBRI + c + 1], scale=0.5)
                    act(scr["aa"][ci][:, 0:w], scr["tr"][ci][:, 0:w], AF.Exp, [("tr", ci), "dv"], [("aa", ci)],
                        bias=dv[:, D_C2 + c:D_C2 + c + 1], scale=dv[:, D_C2 + c:D_C2 + c + 1])
                    act(scr["mm"][bi][:, 0:w], scr["tr"][ci][:, 0:w], AF.Exp, [("tr", ci), "dv"], [("mm", bi)],
                        bias=dv[:, D_C4 + c:D_C4 + c + 1], scale=dv[:, D_C4 + c:D_C4 + c + 1])
                    act(scr["mm"][bi][:, 0:w], scr["mm"][bi][:, 0:w], AF.Relu, [("mm", bi)], [("mm", bi)],
                        bias=1.0 / 16, scale=-1.0 / 16)

                def b_sqrt_act(st):
                    u, w = st["u"], st["w"]
                    for ci, c in enumerate(cs):
                        bi = u * 2 + ci
                        act(scr["mm"][bi][:, 0:w], scr["mm"][bi][:, 0:w], AF.Sqrt, [("mm", bi)], [("mm", bi)])

                def b_tail_ops(st):
                    u, s, c0, w = st["u"], st["s"], st["c0"], st["w"]
                    is_meta = st["meta"]
                    ops = []
                    for ci, c in enumerate(cs):
                        bi = u * 2 + ci
                        hh, hk = scr["xc"][bi], ("xc", bi)
                        ti_t, ti_k = scr["ti"][ci], ("ti", ci)
                        aa_t, aa_k = scr["aa"][ci], ("aa", ci)
                        ops.append(lambda bi=bi, ti_t=ti_t, ti_k=ti_k: stt(ti_t[:, 0:w], ti_t[:, 0:w], 1.0, scr["xc"][bi][:, 0:w],
                                                                           ALU.add, ALU.mult, [ti_k, ("xc", bi)], [ti_k]))
                        if is_meta:
                            ops.append(lambda bi=bi: P.op("dve", lambda e, t=scr["mm"][bi]: e.memset(t[:, 0:1], 0.25), [], [("mm", bi)]))
                        ops.append(lambda bi=bi, ti_t=ti_t, ti_k=ti_k: tt(ti_t[:, 0:w], scr["mm"][bi][:, 0:w], ti_t[:, 0:w], ALU.mult,
                                                                          [("mm", bi), ti_k], [ti_k]))
                        ops.append(lambda c=c, hh=hh, hk=hk, ti_t=ti_t, ti_k=ti_k, aa_t=aa_t, aa_k=aa_k: P.op(
                            "dve", lambda e, o=hh[:, 0:w], a=aa_t[:, 0:w], uu=ti_t[:, 0:w], ini=stH[:, c, :]:
                            e.tensor_tensor_scan(out=o, data0=a, data1=uu, initial=ini, op0=ALU.mult, op1=ALU.add),
                            [aa_k, ti_k, ("stH", c)], [hk]))
                        ops.append(lambda c=c, hh=hh, hk=hk: cp(stH[:, c, :], hh[:, w - 1:w], [hk], [("stH", c)]))
                        if not is_meta:
                            ops.append(lambda bi=bi, c=c, hh=hh, hk=hk: tt(ybT[:, c, c0:c0 + w], hh[:, 0:w], scr["tz"][bi][:, 0:w], ALU.mult,
                                                                           [hk, ("tz", bi)], [("yb", c, s)], eng="pool"))
                    return ops

                def a_pe(c, sl, s, c0, w):
                    ai = unit[0] % 2
                    unit[0] += 1
                    ph, phk = proj_group(sl[2][0], sl[2][1], s, c0, w)
                    pz, pzk = proj_group(sl[3][0], sl[3][1], s, c0, w)
                    pc, pck = proj_group(sl[1][0], sl[1][1], s, c0, w)
                    pb, pbk = proj_group(sl[0][0], sl[0][1], s, c0, w)
                    return dict(ai=ai, c=c, s=s, c0=c0, w=w, pb=(pb, pbk), pc=(pc, pck), ph=(ph, phk), pz=(pz, pzk))

                def a_act(au):
                    ai, w = au["ai"], au["w"]
                    act(scr["cv"][ai][:, 0:w], au["ph"][0][:, 0:w], AF.Copy, [au["ph"][1]], [("cv", ai)])
                    act(scr["tza"][ai][:, 0:w], au["pz"][0][:, 0:w], AF.Tanh, [au["pz"][1]], [("tza", ai)], scale=0.5)

                def a_dve_ops(au, is_meta=False):
                    ai, c, s, c0, w = au["ai"], au["c"], au["s"], au["c0"], au["w"]
                    pb, pbk = au["pb"]
                    pc, pck = au["pc"]
                    pz, pzk = au["pz"]
                    tz_t, tz_k = scr["tza"][ai], ("tza", ai)
                    ch_t, ch_k = scr["ch"][ai], ("ch", ai)
                    cv_t, cv_k = scr["cv"][ai], ("cv", ai)
                    ops = []
                    ops.append(lambda: cp(ch_t[:, 0:2], stA[:, c, :], [("stA", c)], [ch_k]))
                    ops.append(lambda: tt(ch_t[:, 2:2 + w], pc[:, 0:w], cv_t[:, 0:w], ALU.mult, [pck, cv_k], [ch_k]))
                    ops.append(lambda: stt(tz_t[:, 0:w], tz_t[:, 0:w], 1.0, pz[:, 0:w], ALU.add, ALU.mult, [tz_k, pzk], [tz_k]))
                    ops.append(lambda: cp(stA[:, c, :], ch_t[:, w:w + 2], [ch_k], [("stA", c)]))
                    if is_meta:
                        return [ops[0], ops[1], ops[3]]
                    ops.append(lambda: tt(tz_t[:, 0:w], tz_t[:, 0:w], pb[:, 0:w], ALU.mult, [tz_k, pbk], [tz_k]))

                    def pool_part():
                        at_t, at_k = scr["atmp"][0], ("atmp", 0)
                        ts(cv_t[:, 0:w], ch_t[:, 0:w], dv[:, D_CAWH + c:D_CAWH + c + 1], zcol, ALU.mult, ALU.add,
                           [ch_k, "dv", "dvz"], [cv_k], eng="pool")
                        for k in range(1, 3):
                            ts(at_t[:, 0:w], ch_t[:, k:k + w], dv[:, D_CAWH + 8 * k + c:D_CAWH + 8 * k + c + 1], zcol, ALU.mult, ALU.add,
                               [ch_k, "dv", "dvz"], [at_k], eng="pool")
                            tt(cv_t[:, 0:w], cv_t[:, 0:w], at_t[:, 0:w], ALU.add, [cv_k, at_k], [cv_k], eng="pool")
                        tt(yaT[:, c, c0:c0 + w], cv_t[:, 0:w], tz_t[:, 0:w], ALU.mult, [cv_k, tz_k], [("ya", c, s)], eng="pool")
                    ops.append(pool_part)
                    return ops

                if meta_sub is not None:
                    bst.append(b_s123(meta_sub[0], meta_sub[1], meta_sub[2], True))
                    for c in cs:
                        act(mtB[:, c, :], stB[:, c, :], AF.Copy, [("stB", c)], [("mtB", c)])
                for (s, c0, w) in subtiles:
                    bst.append(b_s123(s, c0, w, False))
                blocks_done(4)

                a_units = [(c, si) for c in cs for si in range(len(subtiles))]
                sl = None
                for idx, (c, si) in enumerate(a_units):
                    if si == 0:
                        sl = [next_block(g * 8 + c) for g in range(4)]
                        if meta_sub is not None:
                            mau = a_pe(c, sl, meta_sub[0], meta_sub[1], meta_sub[2])
                            a_act(mau)
                            interleave(a_dve_ops(mau, True))
                            cp(mtA[:, c, :], stA[:, c, :], [("stA", c)], [("mtA", c)])
                    s, c0, w = subtiles[si]
                    bq = idx - (len(a_units) - len(bst))
                    has_b = 0 <= bq < len(bst)
                    if has_b:
                        b_gates_pe(bst[bq])
                    au = a_pe(c, sl, s, c0, w)
                    if has_b:
                        b_gates_act(bst[bq], 0)
                        a_act(au)
                        b_gates_act(bst[bq], 1)
                        b_sqrt_act(bst[bq])
                        interleave(a_dve_ops(au))
                        interleave(b_tail_ops(bst[bq]))
                        if bst[bq]["meta"]:
                            for c_ in cs:
                                cp(mtH[:, c_, :], stH[:, c_, :], [("stH", c_)], [("mtH", c_)])
                    else:
                        a_act(au)
                        interleave(a_dve_ops(au))
                    if si == len(subtiles) - 1:
                        blocks_done(4)

        def phase2(subtiles):
            for j in range(KC):
                sga = next_block(48 + j)
                sgb = next_block(56 + j)
                spa = next_block(64 + j)
                spb = next_block(72 + j)
                for (s, c0, w) in subtiles:
                    bi = unit[0] % 2
                    unit[0] += 1
                    pga, pgak = proj_group(sga[0], sga[1], s, c0, w)
                    pgb, pgbk = proj_group(sgb[0], sgb[1], s, c0, w)
                    pya, pyak = psum()
                    for kc in range(KC):
                        mm(pya[:, 0:w], spa[0][:, kc, :], yaT[:, kc, c0:c0 + w], kc == 0, kc == KC - 1,
                           [spa[1], ("ya", kc, s)], [pyak])
                    pyb, pybk = psum()
                    for kc in range(KC):
                        mm(pyb[:, 0:w], spb[0][:, kc, :], ybT[:, kc, c0:c0 + w], kc == 0, kc == KC - 1,
                           [spb[1], ("yb", kc, s)], [pybk])
                    tga, tgak = scr["tr"][bi], ("tr", bi)
                    tgb, tgbk = scr["ti"][bi], ("ti", bi)
                    act(tga[:, 0:w], pga[:, 0:w], AF.Tanh, [pgak, "dv"], [tgak], bias=dv[:, D_HBG + j:D_HBG + j + 1], scale=0.5)
                    act(tgb[:, 0:w], pgb[:, 0:w], AF.Tanh, [pgbk, "dv"], [tgbk], bias=dv[:, D_HBG + 8 + j:D_HBG + 8 + j + 1], scale=0.5)
                    stt(tga[:, 0:w], tga[:, 0:w], 1.0, pya[:, 0:w], ALU.add, ALU.mult, [tgak, pyak], [tgak])
                    stt(tgb[:, 0:w], tgb[:, 0:w], 1.0, pyb[:, 0:w], ALU.add, ALU.mult, [tgbk, pybk], [tgbk])
                    tt(mT[:, j, c0:c0 + w], tga[:, 0:w], tgb[:, 0:w], ALU.add, [tgak, tgbk], [("m", j, s)])
                blocks_done(4)

        kmA = [("mtA", c) for c in range(KC)]
        kmB = [("mtB", c) for c in range(KC)]
        kmH = [("mtH", c) for c in range(KC)]
        prologue_weights()
        run_boundary([], [dict(kind="IN", g=0, n=NMETA, src=meta_d, c0=T, xk=("xn", "m"))])
        issue_loads()
        subtiles = [(s, s * SW, SW) for s in range(NS)]
        seq_tiles = tiles
        run_boundary([], in_items(seq_tiles[0][1], seq_tiles[0][2]))
        pre_next = {}
        for ti, (_, b, n) in enumerate(seq_tiles):
            if n == 0 and ti > 0:
                cp(stA[:], mtA[:], kmA, kA)
                cp(stB[:], mtB[:], kmB, kB)
                cp(stH[:], mtH[:], kmH, kH)
            phase1(subtiles, meta_sub=(("m", T, NMETA) if ti == 0 else None), pre=(pre_next if ti > 0 else None))
            phase2(subtiles)
            p3 = p3_items(b, n)
            nin = in_items(seq_tiles[ti + 1][1], seq_tiles[ti + 1][2]) if ti + 1 < len(seq_tiles) else []
            pre_next = {}
            run_boundary(p3, nin, tail_hook=((lambda d=pre_next: d.update(pre_project(subtiles))) if nin else None))
        P.op("sp", None, writes=[("xt", i) for i in range(NXT)])

        P.finalize()
        csem = {e: es.enter_context(nc.semaphore("c_" + e)) for e in Prog.ENGS}
        dsem = {k: es.enter_context(nc.semaphore("d_" + k)) for k in sorted(P.dma_cnt.keys())}
        with nc.Block() as block:
            @block.tensor
            def _(e):
                P.emit("pe", e, csem, dsem)

            @block.scalar
            def _(e):
                P.emit("act", e, csem, dsem)

            @block.vector
            def _(e):
                P.emit("dve", e, csem, dsem)

            @block.gpsimd
            def _(e):
                P.emit("pool", e, csem, dsem)

            @block.sync
            def _(e):
                P.emit("sp", e, csem, dsem)
    return nc


def _weights_map(inputs):
    f = lambda a: np.ascontiguousarray(np.asarray(a, dtype=np.float32))
    return {
        "meta": f(inputs["meta"]),
        "norm_g": f(inputs["norm_g"]).reshape(1, D),
        "w_in": f(inputs["w_in"]).reshape(D, 8 * D),
        "b_gate": f(inputs["b_gate"]).reshape(2, D),
        "conv_a_w": f(inputs["conv_a_w"]).reshape(3, D),
        "w_proj_a": f(inputs["w_proj_a"]).reshape(D, D),
        "conv_b_w": f(inputs["conv_b_w"]).reshape(4, D),
        "conv_b_b": f(inputs["conv_b_b"]).reshape(1, D),
        "w_rg_a": f(inputs["w_rg_a"]).reshape(4, 256, 256),
        "b_rg_a": f(inputs["b_rg_a"]).reshape(1, D),
        "w_rg_i": f(inputs["w_rg_i"]).reshape(4, 256, 256),
        "b_rg_i": f(inputs["b_rg_i"]).reshape(1, D),
        "lru_param": f(inputs["lru_param"]).reshape(1, D),
        "w_proj_b": f(inputs["w_proj_b"]).reshape(D, D),
        "w_out": f(inputs["w_out"]).reshape(D, D),
        "final_norm_g": f(inputs["final_norm_g"]).reshape(1, D),
    }


def run(inputs, n_cores, T=1024, trace=False):
    x = np.asarray(inputs["x"], dtype=np.float32)
    B, SEQ, _ = x.shape
    assert B % n_cores == 0
    NSEQ = B // n_cores
    nc = build_nc(NSEQ, SEQ, T)
    wm = _weights_map(inputs)
    in_maps = []
    for c in range(n_cores):
        m = dict(wm)
        m["x"] = np.ascontiguousarray(x[c * NSEQ:(c + 1) * NSEQ])
        in_maps.append(m)
    res = run_bass_kernel_spmd(nc, in_maps, core_ids=list(range(n_cores)), trace=trace)
    out = np.concatenate([np.asarray(r["out"]) for r in res.results], axis=0)
    return out.astype(np.float32), res


def kernel(**inputs):
    out, _ = run(inputs, N_CORES)
    return out
```

```python
import contextlib
import numpy as np
import concourse.bass as bass
import concourse.mybir as mybir
from concourse.bass_utils import run_bass_kernel_spmd

F32 = mybir.dt.float32
BF16 = mybir.dt.bfloat16
AF = mybir.ActivationFunctionType
ALU = mybir.AluOpType

D = 1024
KC = 8
SW = 512
NMETA = 16
EPS = 1e-6
N_CORES = 8


class _Op:
    __slots__ = ("eng", "fn", "pos", "dma", "dval", "waits", "targeted", "seq")


class Prog:
    ENGS = ("pe", "act", "dve", "pool", "sp")

    def __init__(self):
        self.by_eng = {e: [] for e in self.ENGS}
        self.last_w = {}
        self.readers = {}
        self.dma_cnt = {}
        self.waited_c = {}
        self.waited_d = {}

    def op(self, eng, fn, reads=(), writes=(), dma=None):
        o = _Op()
        o.eng, o.fn, o.dma = eng, fn, dma
        o.targeted, o.seq, o.dval = False, None, None
        lst = self.by_eng[eng]
        o.pos = len(lst)
        deps = {}
        for k in reads:
            w = self.last_w.get(k)
            if w is not None:
                deps[id(w)] = (w, True)
        for k in writes:
            w = self.last_w.get(k)
            if w is not None and id(w) not in deps:
                deps[id(w)] = (w, False)
            for r in self.readers.get(k, ()):
                if id(r) not in deps:
                    deps[id(r)] = (r, False)
        waits = []
        dmax = {}
        for d, raw in deps.values():
            if d.dma is not None and (d.dma not in dmax or dmax[d.dma].dval < d.dval):
                dmax[d.dma] = d
        for d in dmax.values():
            key = (eng, d.dma)
            if self.waited_d.get(key, 0) >= d.dval:
                continue
            self.waited_d[key] = d.dval
            waits.append(d)
        for d, raw in deps.values():
            if d is o or d.dma is not None:
                continue
            if d.eng == eng and dma is None:
                if eng == "pe" or not raw:
                    continue
            key = (eng, d.eng)
            if self.waited_c.get(key, -1) >= d.pos:
                continue
            self.waited_c[key] = d.pos
            d.targeted = True
            waits.append(d)
        o.waits = waits
        if dma is not None:
            self.dma_cnt[dma] = self.dma_cnt.get(dma, 0) + 16
            o.dval = self.dma_cnt[dma]
        for k in reads:
            self.readers.setdefault(k, []).append(o)
        for k in writes:
            self.last_w[k] = o
            self.readers[k] = []
        lst.append(o)
        return o

    def finalize(self):
        for e in self.ENGS:
            n = 0
            for o in self.by_eng[e]:
                if o.dma is None and o.targeted:
                    n += 1
                    o.seq = n

    def emit(self, eng, handle, csem, dsem):
        for o in self.by_eng[eng]:
            for d in o.waits:
                if d.dma is not None:
                    handle.wait_ge(dsem[d.dma], d.dval)
                else:
                    handle.wait_ge(csem[d.eng], d.seq)
            if o.fn is None:
                continue
            ins = o.fn(handle)
            if o.dma is not None:
                ins.then_inc(dsem[o.dma], 16)
            elif o.targeted:
                ins.then_inc(csem[o.eng], 1)


CAW, CBW, CBB, BRA, BRI, LAM, BG = 0, 24, 56, 64, 72, 80, 88
NV = 104
D_CAWH, D_HBRA, D_HBRI, D_HBG, D_C2, D_C4, D_TMP, D_ZERO = 0, 24, 32, 40, 56, 64, 72, 88
NDV = 92

NSLOT = 8
NXT = 5


def build_nc(NSEQ, SEQ, T):
    assert SEQ % T == 0 and T % SW == 0
    NS = T // SW
    NG = T // 128
    nc = bass.Bass("TRN2", target_bir_lowering=False)
    din = lambda name, shape: nc.dram_tensor(name, shape, F32, kind="ExternalInput").ap()
    x_d = din("x", [NSEQ, SEQ, D])
    meta_d = din("meta", [NMETA, D])
    norm_g_d = din("norm_g", [1, D])
    w_in_d = din("w_in", [D, 8 * D])
    b_gate_d = din("b_gate", [2, D])
    conv_a_w_d = din("conv_a_w", [3, D])
    w_proj_a_d = din("w_proj_a", [D, D])
    conv_b_w_d = din("conv_b_w", [4, D])
    conv_b_b_d = din("conv_b_b", [1, D])
    w_rg_a_d = din("w_rg_a", [4, 256, 256])
    b_rg_a_d = din("b_rg_a", [1, D])
    w_rg_i_d = din("w_rg_i", [4, 256, 256])
    b_rg_i_d = din("b_rg_i", [1, D])
    lru_d = din("lru_param", [1, D])
    w_proj_b_d = din("w_proj_b", [D, D])
    w_out_d = din("w_out", [D, D])
    fg_d = din("final_norm_g", [1, D])
    out_d = nc.dram_tensor("out", [NSEQ, SEQ, D], F32, kind="ExternalOutput").ap()
    wsc = nc.dram_tensor("wsc", [80, 128, D], BF16, kind="Internal").ap()

    P = Prog()
    es = contextlib.ExitStack()
    with es:
        sb = lambda name, shape, dt: es.enter_context(nc.sbuf_tensor(name, shape, dt))
        xnT = sb("xnT", [128, KC, T + NMETA], BF16)
        yaT = sb("yaT", [128, KC, T], BF16)
        ybT = sb("ybT", [128, KC, T], BF16)
        mT = sb("mT", [128, KC, T], BF16)
        wo = sb("wo", [128, KC, D], BF16)
        wrg = sb("wrg", [128, 2, 8, 256], BF16)
        gbc = sb("gbc", [128, D], F32)
        fgbc = sb("fgbc", [128, D], F32)
        slots = [sb(f"slot{i}", [128, KC, 128], BF16) for i in range(NSLOT)]
        xt = [sb(f"xt{i}", [128, D], F32) for i in range(NXT)]
        xs = [sb(f"xs{i}", [128, D], BF16) for i in range(3)]
        junk = sb("junk", [128, D], BF16)
        xcb = [sb(f"xcb{i}", [128, 2, SW], BF16) for i in range(2)]
        vecs = sb("vecs", [NV, 128], F32)
        vT = sb("vT", [128, NV], F32)
        dv = sb("dv", [128, NDV], F32)
        identf = sb("identf", [128, 128], F32)
        identb = sb("identb", [128, 128], BF16)
        mhalf = sb("mhalf", [128, 1], F32)
        stA = sb("stA", [128, KC, 2], F32)
        stB = sb("stB", [128, KC, 3], F32)
        stH = sb("stH", [128, KC, 1], F32)
        mtA = sb("mtA", [128, KC, 2], F32)
        mtB = sb("mtB", [128, KC, 3], F32)
        mtH = sb("mtH", [128, KC, 1], F32)
        sq = [sb(f"sq{i}", [128, 4], F32) for i in range(4)]
        SCW = SW + 4
        scr = {}

        def scratch(name, n):
            scr[name] = [sb(f"{name}{i}", [128, SCW], F32) for i in range(n)]

        scratch("xbuf", 4)
        scratch("xc", 4)
        scratch("tr", 2)
        scratch("ti", 2)
        scratch("aa", 2)
        scratch("mm", 4)
        scratch("tz", 4)
        scratch("tza", 2)
        scratch("ch", 2)
        scratch("cv", 2)
        scratch("atmp", 1)
        for nm in ("xbuf", "xc", "mm", "tz"):
            scr[nm] += [sb(f"{nm}M{i}", [128, NMETA + 4], F32) for i in range(2)]
        xcb.append(sb("xcbM", [128, 2, NMETA], BF16))
        ps = [es.enter_context(nc.psum_tensor(f"ps{i}", [128, SW], F32)) for i in range(8)]

        cnt = {"ps": 0, "slot": 0, "sq": 0, "xs": 0}

        def psum():
            i = cnt["ps"] % 8
            cnt["ps"] += 1
            return ps[i], ("ps", i)

        def dma(out, in_, reads, writes, key):
            P.op("sp", lambda e: e.dma_start(out=out, in_=in_), reads, writes, dma=key)

        def act(out, in_, func, reads, writes, bias=None, scale=None, accum_out=None):
            kw = {}
            if bias is not None:
                kw["bias"] = bias
            if scale is not None:
                kw["scale"] = scale
            if accum_out is not None:
                kw["accum_out"] = accum_out
            P.op("act", lambda e: e.activation(out=out, in_=in_, func=func, **kw), reads, writes)

        def tt(out, in0, in1, op, reads, writes, eng="dve"):
            P.op(eng, lambda e: e.tensor_tensor(out=out, in0=in0, in1=in1, op=op), reads, writes)

        def ts(out, in0, s1, s2, op0, op1, reads, writes, eng="dve"):
            if op1 is None:
                P.op(eng, lambda e: e.tensor_scalar(out=out, in0=in0, scalar1=s1, scalar2=None, op0=op0), reads, writes)
            else:
                P.op(eng, lambda e: e.tensor_scalar(out=out, in0=in0, scalar1=s1, scalar2=s2, op0=op0, op1=op1), reads, writes)

        def stt(out, in0, scalar, in1, op0, op1, reads, writes):
            P.op("dve", lambda e: e.scalar_tensor_tensor(out=out, in0=in0, scalar=scalar, in1=in1, op0=op0, op1=op1), reads, writes)

        def cp(out, in_, reads, writes, eng="dve"):
            P.op(eng, lambda e: e.tensor_copy(out=out, in_=in_), reads, writes)

        def mm(out, lhsT, rhs, start, stop, reads, writes):
            P.op("pe", lambda e: e.matmul(out, lhsT=lhsT, rhs=rhs, start=start, stop=stop), reads, writes)

        def tr_(out, in_, ident, reads, writes):
            P.op("pe", lambda e: e.transpose(out=out, in_=in_, identity=ident), reads, writes)

        ya_keys = [("ya", j, s) for j in range(KC) for s in range(NS)]
        yb_keys = [("yb", j, s) for j in range(KC) for s in range(NS)]
        m_keys = [("m", j, s) for j in range(KC) for s in range(NS)]

        P.op("pool", lambda e: e.memset(identf[:], 0.0), writes=["identf"])
        P.op("pool", lambda e: e.affine_select(out=identf[:], in_=identf[:], compare_op=ALU.not_equal, fill=1.0, base=0,
                                               pattern=[[-1, 128]], channel_multiplier=1), reads=["identf"], writes=["identf"])
        P.op("pool", lambda e: e.memset(mhalf[:], -0.5), writes=["mhalf"])
        kA = [("stA", c) for c in range(KC)]
        kB = [("stB", c) for c in range(KC)]
        kH = [("stH", c) for c in range(KC)]
        P.op("pool", lambda e: e.memset(stA[:], 0.0), writes=kA)
        P.op("pool", lambda e: e.memset(stB[:], 0.0), writes=kB)
        P.op("pool", lambda e: e.memset(stH[:], 0.0), writes=kH)
        cp(identb[:], identf[:], ["identf"], ["identb"])
        P.op("pool", lambda e: e.memset(dv[:, D_ZERO:D_ZERO + 1], 0.0), writes=["dvz"])
        vec_srcs = [(conv_a_w_d, CAW, 24), (conv_b_w_d, CBW, 32), (conv_b_b_d, CBB, 8), (b_rg_a_d, BRA, 8),
                    (b_rg_i_d, BRI, 8), (lru_d, LAM, 8), (b_gate_d, BG, 16)]
        for i, (src, r0, n) in enumerate(vec_srcs):
            dma(vecs[r0:r0 + n, :], src.rearrange("k (j p) -> (k j) p", p=128), [], [("vecs", i)], "cst")
        dma(gbc[:], norm_g_d.partition_broadcast(128), [], ["gbc"], "cstg")
        dma(fgbc[:], fg_d.partition_broadcast(128), [], ["fgbc"], "cstf")
        pst, pk = psum()
        tr_(pst[:, 0:NV], vecs[:], identf[0:NV, 0:NV], [("vecs", i) for i in range(7)] + ["identf"], [pk])
        cp(vT[:], pst[:, 0:NV], [pk], ["vT"])
        ts(dv[:, D_CAWH:D_CAWH + 24], vT[:, CAW:CAW + 24], 0.5, None, ALU.mult, None, ["vT"], ["dv"])
        ts(dv[:, D_HBRA:D_HBRA + 16], vT[:, BRA:BRA + 16], 0.5, None, ALU.mult, None, ["vT"], ["dv"])
        ts(dv[:, D_HBG:D_HBG + 16], vT[:, BG:BG + 16], 0.5, None, ALU.mult, None, ["vT"], ["dv"])
        act(dv[:, D_TMP:D_TMP + 8], vT[:, LAM:LAM + 8], AF.Exp, ["vT"], ["dv"], scale=-1.0)
        act(dv[:, D_TMP + 8:D_TMP + 16], dv[:, D_TMP:D_TMP + 8], AF.Ln, ["dv"], ["dv"], bias=1.0, scale=1.0)
        ts(dv[:, D_C2:D_C2 + 8], dv[:, D_TMP + 8:D_TMP + 16], -4.0, None, ALU.mult, None, ["dv"], ["dv"])
        ts(dv[:, D_C4:D_C4 + 8], dv[:, D_TMP + 8:D_TMP + 16], -8.0, None, ALU.mult, None, ["dv"], ["dv"])

        xtc = {"n": 0}

        def rstd_act(sqt, n, sqk):
            act(sqt[0:n, 1:2], sqt[0:n, 0:1], AF.Sqrt, [sqk], [sqk], bias=EPS, scale=1.0 / D)

        def rstd_dve(sqt, n, sqk):
            P.op("dve", lambda e: e.reciprocal(out=sqt[0:n, 2:3], in_=sqt[0:n, 1:2]), [sqk], [sqk])

        def item_load(it):
            bi = xtc["n"] % NXT
            xtc["n"] += 1
            it["xb"] = bi
            n = it["n"]
            dma(xt[bi][0:n, :], it["src"], [], [("xt", bi)], f"xl{bi}")

        def new_sq(it):
            sqi = cnt["sq"] % 4
            cnt["sq"] += 1
            it["sq"] = (sq[sqi], ("sq", sqi))

        def in_A_act(it):
            n, bi = it["n"], it["xb"]
            new_sq(it)
            sqt, sqk = it["sq"]
            act(junk[0:n, :], xt[bi][0:n, :], AF.Square, [("xt", bi)], [sqk], accum_out=sqt[0:n, 0:1])
            rstd_act(sqt, n, sqk)

        def in_A_pool(it):
            sqt, sqk = it["sq"]
            rstd_dve(sqt, it["n"], sqk)

        def in_B1_dve(it):
            n, bi = it["n"], it["xb"]
            sqt, sqk = it["sq"]
            ts(xt[bi][0:n, :], xt[bi][0:n, :], sqt[0:n, 2:3], None, ALU.mult, None, [("xt", bi), sqk], [("xt", bi)])

        def in_B1_pool(it):
            n, bi = it["n"], it["xb"]
            jb = cnt["xs"] % 3
            cnt["xs"] += 1
            xsb, xsk = xs[jb], ("xs", jb)
            tt(xsb[0:n, :], xt[bi][0:n, :], gbc[0:n, :], ALU.mult, [("xt", bi), "gbc"], [xsk], eng="pool")
            it["xs"] = (xsb, xsk)

        def in_B2(it):
            n = it["n"]
            xsb, xsk = it["xs"]
            pt, pk = psum()
            ptb = pt[:].bitcast(BF16)
            for kc in range(KC):
                tr_(ptb[:, kc * 128:kc * 128 + n], xsb[0:n, kc * 128:(kc + 1) * 128], identb[0:n, 0:n], [xsk, "identb"], [pk])
            it["pt"] = (ptb, pk)

        def in_C(it):
            n, g = it["n"], it["g"]
            ptb, pk = it["pt"]
            c0 = it.get("c0", g * 128)
            xk = it.get("xk", ("xn", (g * 128) // SW))
            act(xnT[:, :, c0:c0 + n], ptb.rearrange("p (k t) -> p k t", k=KC)[:, :, 0:n], AF.Copy, [pk], [xk])

        def p3_A_pe(it):
            g = it["g"]
            s_ = (g * 128) // SW
            it["po"] = []
            for half in range(2):
                po, pok = psum()
                for kc in range(KC):
                    mm(po[:], mT[:, kc, g * 128:(g + 1) * 128], wo[:, kc, half * SW:(half + 1) * SW], kc == 0, kc == KC - 1,
                       [("m", kc, s_), "wo"], [pok])
                it["po"].append((po, pok))

        def p3_A_dve(it, half):
            bi = it["xb"]
            xtb, xtk = xt[bi], ("xt", bi)
            if True:
                po, pok = it["po"][half]
                stt(xtb[:, half * SW:(half + 1) * SW], po[:], 0.5, xtb[:, half * SW:(half + 1) * SW], ALU.mult, ALU.add,
                    [pok, xtk], [xtk])

        def p3_B_act(it):
            bi = it["xb"]
            new_sq(it)
            sqt, sqk = it["sq"]
            act(junk[:], xt[bi][:], AF.Square, [("xt", bi)], [sqk], accum_out=sqt[:, 0:1])
            rstd_act(sqt, 128, sqk)

        def p3_B_pool(it):
            sqt, sqk = it["sq"]
            rstd_dve(sqt, 128, sqk)

        def p3_C(it):
            bi = it["xb"]
            xtb, xtk = xt[bi], ("xt", bi)
            sqt, sqk = it["sq"]
            stt(xtb[:], xtb[:], sqt[:, 2:3], fgbc[:], ALU.mult, ALU.mult, [xtk, sqk, "fgbc"], [xtk])
            dma(it["dst"], xtb[:], [xtk], [], f"st{bi}")

        def prefetch_boundary(p3, nin, k=4):
            items = []
            for i in range(max(len(p3), len(nin))):
                if i < len(p3):
                    items.append(p3[i])
                if i < len(nin):
                    items.append(nin[i])
            for it in items[:min(k, NXT - 1)]:
                item_load(it)
                it["loaded"] = True

        def run_boundary(p3, nin, tail_hook=None):
            n = max(len(p3), len(nin))
            loads = []
            for i in range(n):
                if i < len(p3):
                    loads.append(p3[i])
                if i < len(nin):
                    loads.append(nin[i])
            nl = [0]
            for it in loads:
                it["fin"] = False

            def load_more(maxahead):
                while nl[0] < len(loads) and nl[0] < maxahead:
                    k = nl[0]
                    if loads[k].get("loaded"):
                        nl[0] += 1
                        continue
                    if k >= NXT and not loads[k - NXT]["fin"]:
                        break
                    item_load(loads[k])
                    nl[0] += 1
            per = (1 if p3 else 0) + (1 if nin else 0)
            P_ = lambda k: p3[k] if 0 <= k < len(p3) else None
            I_ = lambda k: nin[k] if 0 <= k < len(nin) else None
            for t in range(n + 3):
                if t == n and tail_hook is not None:
                    tail_hook()
                load_more(per * (t + 2))
                if I_(t - 2):
                    in_B2(I_(t - 2))
                if P_(t):
                    p3_A_pe(P_(t))
                if P_(t - 1):
                    p3_B_act(P_(t - 1))
                if I_(t):
                    in_A_act(I_(t))
                if I_(t - 2):
                    in_C(I_(t - 2))
                if I_(t - 1):
                    in_B1_dve(I_(t - 1))
                    in_B1_pool(I_(t - 1))
                    I_(t - 1)["fin"] = True
                if P_(t - 1):
                    p3_B_pool(P_(t - 1))
                load_more(per * (t + 2) + 1)
                if P_(t):
                    p3_A_dve(P_(t), 0)
                if P_(t - 1):
                    p3_C(P_(t - 1))
                    P_(t - 1)["fin"] = True
                if I_(t):
                    in_A_pool(I_(t))
                if P_(t):
                    p3_A_dve(P_(t), 1)

        def in_items(b, n):
            return [dict(kind="IN", g=g, n=128, src=x_d[b, n * T + g * 128:n * T + (g + 1) * 128, :]) for g in range(NG)]

        def p3_items(b, n):
            return [dict(kind="P3", g=g, n=128, src=x_d[b, n * T + g * 128:n * T + (g + 1) * 128, :],
                         dst=out_d[b, n * T + g * 128:n * T + (g + 1) * 128, :]) for g in range(NG)]

        stg = [yaT[:].bitcast(F32), ybT[:].bitcast(F32), xnT[:].bitcast(F32)[:, :, 0:T // 2]]
        stg_keys = [ya_keys, yb_keys, [("xn", s) for s in range(NS)] + [("xn", "m")]]
        NSTG = 3
        mflat = mT[:].rearrange("p k t -> p (k t)")
        wb = [mflat[:, i * 4096:(i + 1) * 4096].rearrange("p (j k c) -> p j k c", j=4, k=KC) for i in range(2)]
        wb_keys = [[("m", j, s) for j in range(4 * i, 4 * i + 4) for s in range(NS)] for i in range(2)]
        def prologue_weights():
            rounds = []
            for g in (4, 5, 0, 1, 2, 3, 6, 7):
                for jh in range(2):
                    c0 = g * D + jh * 512
                    rounds.append(dict(src=w_in_d[:, c0:c0 + 512].rearrange("(kc p) n -> p kc n", p=128), blk0=g * 8 + jh * 4))
            for jh in range(2):
                rounds.append(dict(src=w_proj_a_d[:, jh * 512:(jh + 1) * 512].rearrange("(kc p) n -> p kc n", p=128), blk0=64 + jh * 4))
            for jh in range(2):
                rounds.append(dict(src=w_proj_b_d[:, jh * 512:(jh + 1) * 512].rearrange("(kc p) n -> p kc n", p=128), blk0=72 + jh * 4))
            for jh in range(2):
                rounds.append(dict(src=w_out_d[:, jh * 512:(jh + 1) * 512].rearrange("(kc p) n -> p kc n", p=128),
                                   sb=wo[:, :, jh * 512:(jh + 1) * 512], sbk=["wo"]))
            for gi, wsrc in enumerate((w_rg_a_d, w_rg_i_d)):
                rounds.append(dict(src=wsrc.rearrange("h (kc p) n -> p (h kc) n", p=128), sb=wrg[:, gi, :, :], sbk=["wrg"], ncol=256))

            def do_load(r):
                rd = rounds[r]
                i = r % NSTG
                dst = stg[i] if "ncol" not in rd else stg[i][:, :, 0:rd["ncol"]]
                dma(dst, rd["src"], [], stg_keys[i], f"pl{i}")

            nwb = [0]

            def do_cast_store(r):
                rd = rounds[r]
                i = r % NSTG
                if "blk0" in rd:
                    wi = nwb[0] % 2
                    nwb[0] += 1
                    o_ap = wb[wi].rearrange("p j k c -> p k j c")
                    i_ap = stg[i].rearrange("p k (j c) -> p k j c", j=4)
                    cp(o_ap, i_ap, stg_keys[i], wb_keys[wi])
                    b0 = rd["blk0"]
                    dma(wsc[b0:b0 + 4].rearrange("b p n -> p b n"), wb[wi].rearrange("p j k c -> p j (k c)"),
                        wb_keys[wi], [("wsc", b0 + q) for q in range(4)], f"pst{wi}")
                else:
                    src = stg[i] if "ncol" not in rd else stg[i][:, :, 0:rd["ncol"]]
                    act(rd["sb"], src, AF.Copy, stg_keys[i], rd["sbk"])

            for r in range(min(NSTG - 1, len(rounds))):
                do_load(r)
            for r in range(len(rounds)):
                if r + NSTG - 1 < len(rounds):
                    do_load(r + NSTG - 1)
                do_cast_store(r)

        tiles = [("seq", b, n) for b in range(NSEQ) for n in range(SEQ // T)]
        seq_blocks = []
        for t in tiles:
            for h in range(4):
                for g in (4, 5):
                    for c in (2 * h, 2 * h + 1):
                        seq_blocks.append(g * 8 + c)
                for c in (2 * h, 2 * h + 1):
                    for g in (0, 1, 2, 3):
                        seq_blocks.append(g * 8 + c)
            if t[0] != "meta":
                for j in range(KC):
                    seq_blocks += [48 + j, 56 + j, 64 + j, 72 + j]
        bs = {"issued": 0, "used": 0, "done": 0}

        def issue_loads():
            upto = min(bs["done"] + NSLOT, len(seq_blocks))
            while bs["issued"] < upto:
                k = bs["issued"]
                blk = seq_blocks[k]
                si = k % NSLOT
                dma(slots[si][:].rearrange("p k c -> p (k c)"), wsc[blk], [("wsc", blk)], [("slot", si)], f"wl{si}")
                bs["issued"] += 1

        def next_block(blk):
            k = bs["used"]
            assert seq_blocks[k] == blk, (k, seq_blocks[k], blk)
            assert k < bs["issued"], (k, bs["issued"])
            bs["used"] += 1
            si = k % NSLOT
            return slots[si], ("slot", si)

        def blocks_done(n):
            bs["done"] += n
            assert bs["done"] == bs["used"]
            issue_loads()

        def proj_group(slot, skey, s, c0, w):
            pt, pk = psum()
            for kc in range(KC):
                mm(pt[:, 0:w], slot[:, kc, :], xnT[:, kc, c0:c0 + w], kc == 0, kc == KC - 1, [skey, ("xn", s)], [pk])
            return pt, pk

        unit = [0]
        zcol = dv[:, D_ZERO:D_ZERO + 1]

        def interleave(*lists):
            out = []
            n = max(len(l) for l in lists)
            for i in range(n):
                for l in lists:
                    if i < len(l):
                        out.append(l[i])
            for f in out:
                f()

        def pre_project(subtiles):
            cs = (0, 1)
            sl_xb = [next_block(4 * 8 + c) for c in cs]
            sl_zb = [next_block(5 * 8 + c) for c in cs]
            s, c0, w = subtiles[0]
            pz = [proj_group(sl_zb[ci][0], sl_zb[ci][1], s, c0, w) for ci in range(2)]
            pt = [proj_group(sl_xb[ci][0], sl_xb[ci][1], s, c0, w) for ci in range(2)]
            return dict(sl_xb=sl_xb, sl_zb=sl_zb, pz=pz, pt=pt, s=s)

        def phase1(subtiles, meta_sub=None, pre=None):
            for h in range(4):
                cs = (2 * h, 2 * h + 1)
                if h == 0 and pre:
                    sl_xb, sl_zb = pre["sl_xb"], pre["sl_zb"]
                else:
                    sl_xb = [next_block(4 * 8 + c) for c in cs]
                    sl_zb = [next_block(5 * 8 + c) for c in cs]
                bst = []

                def b_s123(s, c0, w, is_meta):
                    if is_meta:
                        u = 2
                    else:
                        u = unit[0] % 2
                        unit[0] += 1
                    st = dict(u=u, s=s, c0=c0, w=w, meta=is_meta)
                    xcbt, xcbk = xcb[u], ("xcb", u)
                    for ci, c in enumerate(cs):
                        if is_meta:
                            break
                        bi = u * 2 + ci
                        if h == 0 and pre and pre["s"] == s and not is_meta:
                            pz, pzk = pre["pz"][ci]
                        else:
                            pz, pzk = proj_group(sl_zb[ci][0], sl_zb[ci][1], s, c0, w)
                        act(scr["tz"][bi][:, 0:w], pz[:, 0:w], AF.Tanh, [pzk], [("tz", bi)], scale=0.5)
                        stt(scr["tz"][bi][:, 0:w], scr["tz"][bi][:, 0:w], 1.0, pz[:, 0:w], ALU.add, ALU.mult,
                            [("tz", bi), pzk], [("tz", bi)])
                    for ci, c in enumerate(cs):
                        bi = u * 2 + ci
                        xb_t, xb_k = scr["xbuf"][bi], ("xbuf", bi)
                        xc_t, xc_k = scr["xc"][bi], ("xc", bi)
                        if h == 0 and pre and pre["s"] == s and not is_meta:
                            pt, pk = pre["pt"][ci]
                        else:
                            pt, pk = proj_group(sl_xb[ci][0], sl_xb[ci][1], s, c0, w)
                        act(xb_t[:, 0:3], stB[:, c, :], AF.Copy, [("stB", c)], [xb_k])
                        act(xb_t[:, 3:3 + w], pt[:, 0:w], AF.Copy, [pk], [xb_k])
                        act(stB[:, c, :], xb_t[:, w:w + 3], AF.Copy, [xb_k], [("stB", c)])
                        ts(xc_t[:, 0:w], xb_t[:, 0:w], vT[:, CBW + c:CBW + c + 1], vT[:, CBB + c:CBB + c + 1],
                           ALU.mult, ALU.add, [xb_k, "vT"], [xc_k])
                        for k in range(1, 4):
                            stt(xc_t[:, 0:w], xb_t[:, k:k + w], vT[:, CBW + 8 * k + c:CBW + 8 * k + c + 1], xc_t[:, 0:w],
                                ALU.mult, ALU.add, [xb_k, xc_k, "vT"], [xc_k])
                        cp(xcbt[:, ci, 0:w], xc_t[:, 0:w], [xc_k], [xcbk])
                    return st

                def b_gates_pe(st):
                    u, s, c0, w = st["u"], st["s"], st["c0"], st["w"]
                    xcbt, xcbk = xcb[u], ("xcb", u)
                    gp = []
                    for ci, c in enumerate(cs):
                        pr, prk = psum()
                        for kc in range(2):
                            mm(pr[:, 0:w], wrg[:, 0, h * 2 + kc, ci * 128:(ci + 1) * 128], xcbt[:, kc, 0:w], kc == 0, kc == 1,
                               ["wrg", xcbk], [prk])
                        pi, pik = psum()
                        for kc in range(2):
                            mm(pi[:, 0:w], wrg[:, 1, h * 2 + kc, ci * 128:(ci + 1) * 128], xcbt[:, kc, 0:w], kc == 0, kc == 1,
                               ["wrg", xcbk], [pik])
                        gp.append((pr, prk, pi, pik))
                    st["gp"] = gp

                def b_gates_act(st, ci):
                    u, s, c0, w = st["u"], st["s"], st["c0"], st["w"]
                    c = cs[ci]
                    bi = u * 2 + ci
                    pr, prk, pi, pik = st["gp"][ci]
                    act(scr["tr"][ci][:, 0:w], pr[:, 0:w], AF.Tanh, [prk, "dv"], [("tr", ci)],
                        bias=dv[:, D_HBRA + c:D_HBRA + c + 1], scale=0.5)
                    act(scr["ti"][ci][:, 0:w], pi[:, 0:w], AF.Tanh, [pik, "dv"], [("ti", ci)],
                        bias=dv[:, D_HBRI + c:D_HBRI + c + 1], scale=0.5)
                    act(scr["aa"][ci][:, 0:w], scr["tr"][ci][:, 0:w], AF.Exp, [("tr", ci), "dv"], [("aa", ci)],
                        bias=dv[:, D_C2 + c:D_C2 + c + 1], scale=dv[:, D_C2 + c:D_C2 + c + 1])
                    act(scr["mm"][bi][:, 0:w], scr["tr"][ci][:, 0:w], AF.Exp, [("tr", ci), "dv"], [("mm", bi)],
                        bias=dv[:, D_C4 + c:D_C4 + c + 1], scale=dv[:, D_C4 + c:D_C4 + c + 1])
                    act(scr["mm"][bi][:, 0:w], scr["mm"][bi][:, 0:w], AF.Relu, [("mm", bi)], [("mm", bi)],
                        bias=1.0 / 16, scale=-1.0 / 16)

                def b_sqrt_act(st):
                    u, w = st["u"], st["w"]
                    for ci, c in enumerate(cs):
                        bi = u * 2 + ci
                        act(scr["mm"][bi][:, 0:w], scr["mm"][bi][:, 0:w], AF.Sqrt, [("mm", bi)], [("mm", bi)])

                def b_tail_ops(st):
                    u, s, c0, w = st["u"], st["s"], st["c0"], st["w"]
                    is_meta = st["meta"]
                    ops = []
                    for ci, c in enumerate(cs):
                        bi = u * 2 + ci
                        hh, hk = scr["xc"][bi], ("xc", bi)
                        ti_t, ti_k = scr["ti"][ci], ("ti", ci)
                        aa_t, aa_k = scr["aa"][ci], ("aa", ci)
                        ops.append(lambda bi=bi, ti_t=ti_t, ti_k=ti_k: stt(ti_t[:, 0:w], ti_t[:, 0:w], 1.0, scr["xc"][bi][:, 0:w],
                                                                           ALU.add, ALU.mult, [ti_k, ("xc", bi)], [ti_k]))
                        if is_meta:
                            ops.append(lambda bi=bi: P.op("dve", lambda e, t=scr["mm"][bi]: e.memset(t[:, 0:1], 0.25), [], [("mm", bi)]))
                        ops.append(lambda bi=bi, ti_t=ti_t, ti_k=ti_k: tt(ti_t[:, 0:w], scr["mm"][bi][:, 0:w], ti_t[:, 0:w], ALU.mult,
                                                                          [("mm", bi), ti_k], [ti_k]))
                        ops.append(lambda c=c, hh=hh, hk=hk, ti_t=ti_t, ti_k=ti_k, aa_t=aa_t, aa_k=aa_k: P.op(
                            "dve", lambda e, o=hh[:, 0:w], a=aa_t[:, 0:w], uu=ti_t[:, 0:w], ini=stH[:, c, :]:
                            e.tensor_tensor_scan(out=o, data0=a, data1=uu, initial=ini, op0=ALU.mult, op1=ALU.add),
                            [aa_k, ti_k, ("stH", c)], [hk]))
                        ops.append(lambda c=c, hh=hh, hk=hk: cp(stH[:, c, :], hh[:, w - 1:w], [hk], [("stH", c)]))
                        if not is_meta:
                            ops.append(lambda bi=bi, c=c, hh=hh, hk=hk: tt(ybT[:, c, c0:c0 + w], hh[:, 0:w], scr["tz"][bi][:, 0:w], ALU.mult,
                                                                           [hk, ("tz", bi)], [("yb", c, s)], eng="pool"))
                    return ops

                def a_pe(c, sl, s, c0, w):
                    ai = unit[0] % 2
                    unit[0] += 1
                    ph, phk = proj_group(sl[2][0], sl[2][1], s, c0, w)
                    pz, pzk = proj_group(sl[3][0], sl[3][1], s, c0, w)
                    pc, pck = proj_group(sl[1][0], sl[1][1], s, c0, w)
                    pb, pbk = proj_group(sl[0][0], sl[0][1], s, c0, w)
                    return dict(ai=ai, c=c, s=s, c0=c0, w=w, pb=(pb, pbk), pc=(pc, pck), ph=(ph, phk), pz=(pz, pzk))

                def a_act(au):
                    ai, w = au["ai"], au["w"]
                    act(scr["cv"][ai][:, 0:w], au["ph"][0][:, 0:w], AF.Copy, [au["ph"][1]], [("cv", ai)])
                    act(scr["tza"][ai][:, 0:w], au["pz"][0][:, 0:w], AF.Tanh, [au["pz"][1]], [("tza", ai)], scale=0.5)

                def a_dve_ops(au, is_meta=False):
                    ai, c, s, c0, w = au["ai"], au["c"], au["s"], au["c0"], au["w"]
                    pb, pbk = au["pb"]
                    pc, pck = au["pc"]
                    pz, pzk = au["pz"]
                    tz_t, tz_k = scr["tza"][ai], ("tza", ai)
                    ch_t, ch_k = scr["ch"][ai], ("ch", ai)
                    cv_t, cv_k = scr["cv"][ai], ("cv", ai)
                    ops = []
                    ops.append(lambda: cp(ch_t[:, 0:2], stA[:, c, :], [("stA", c)], [ch_k]))
                    ops.append(lambda: tt(ch_t[:, 2:2 + w], pc[:, 0:w], cv_t[:, 0:w], ALU.mult, [pck, cv_k], [ch_k]))
                    ops.append(lambda: stt(tz_t[:, 0:w], tz_t[:, 0:w], 1.0, pz[:, 0:w], ALU.add, ALU.mult, [tz_k, pzk], [tz_k]))
                    ops.append(lambda: cp(stA[:, c, :], ch_t[:, w:w + 2], [ch_k], [("stA", c)]))
                    if is_meta:
                        return [ops[0], ops[1], ops[3]]
                    ops.append(lambda: tt(tz_t[:, 0:w], tz_t[:, 0:w], pb[:, 0:w], ALU.mult, [tz_k, pbk], [tz_k]))

                    def pool_part():
                        at_t, at_k = scr["atmp"][0], ("atmp", 0)
                        ts(cv_t[:, 0:w], ch_t[:, 0:w], dv[:, D_CAWH + c:D_CAWH + c + 1], zcol, ALU.mult, ALU.add,
                           [ch_k, "dv", "dvz"], [cv_k], eng="pool")
                        for k in range(1, 3):
                            ts(at_t[:, 0:w], ch_t[:, k:k + w], dv[:, D_CAWH + 8 * k + c:D_CAWH + 8 * k + c + 1], zcol, ALU.mult, ALU.add,
                               [ch_k, "dv", "dvz"], [at_k], eng="pool")
                            tt(cv_t[:, 0:w], cv_t[:, 0:w], at_t[:, 0:w], ALU.add, [cv_k, at_k], [cv_k], eng="pool")
                        tt(yaT[:, c, c0:c0 + w], cv_t[:, 0:w], tz_t[:, 0:w], ALU.mult, [cv_k, tz_k], [("ya", c, s)], eng="pool")
                    ops.append(pool_part)
                    return ops

                if meta_sub is not None:
                    bst.append(b_s123(meta_sub[0], meta_sub[1], meta_sub[2], True))
                    for c in cs:
                        act(mtB[:, c, :], stB[:, c, :], AF.Copy, [("stB", c)], [("mtB", c)])
                for (s, c0, w) in subtiles:
                    bst.append(b_s123(s, c0, w, False))
                blocks_done(4)

                a_units = [(c, si) for c in cs for si in range(len(subtiles))]
                sl = None
                for idx, (c, si) in enumerate(a_units):
                    if si == 0:
                        sl = [next_block(g * 8 + c) for g in range(4)]
                        if meta_sub is not None:
                            mau = a_pe(c, sl, meta_sub[0], meta_sub[1], meta_sub[2])
                            a_act(mau)
                            interleave(a_dve_ops(mau, True))
                            cp(mtA[:, c, :], stA[:, c, :], [("stA", c)], [("mtA", c)])
                    s, c0, w = subtiles[si]
                    bq = idx - (len(a_units) - len(bst))
                    has_b = 0 <= bq < len(bst)
                    if has_b:
                        b_gates_pe(bst[bq])
                    au = a_pe(c, sl, s, c0, w)
                    if has_b:
                        b_gates_act(bst[bq], 0)
                        a_act(au)
                        b_gates_act(bst[bq], 1)
                        b_sqrt_act(bst[bq])
                        interleave(a_dve_ops(au))
                        interleave(b_tail_ops(bst[bq]))
                        if bst[bq]["meta"]:
                            for c_ in cs:
                                cp(mtH[:, c_, :], stH[:, c_, :], [("stH", c_)], [("mtH", c_)])
                    else:
                        a_act(au)
                        interleave(a_dve_ops(au))
                    if si == len(subtiles) - 1:
                        blocks_done(4)

        def phase2(subtiles):
            for j in range(KC):
                sga = next_block(48 + j)
                sgb = next_block(56 + j)
                spa = next_block(64 + j)
                spb = next_block(72 + j)
                for (s, c0, w) in subtiles:
                    bi = unit[0] % 2
                    unit[0] += 1
                    pga, pgak = proj_group(sga[0], sga[1], s, c0, w)
                    pgb, pgbk = proj_group(sgb[0], sgb[1], s, c0, w)
                    pya, pyak = psum()
                    for kc in range(KC):
                        mm(pya[:, 0:w], spa[0][:, kc, :], yaT[:, kc, c0:c0 + w], kc == 0, kc == KC - 1,
                           [spa[1], ("ya", kc, s)], [pyak])
                    pyb, pybk = psum()
                    for kc in range(KC):
                        mm(pyb[:, 0:w], spb[0][:, kc, :], ybT[:, kc, c0:c0 + w], kc == 0, kc == KC - 1,
                           [spb[1], ("yb", kc, s)], [pybk])
                    tga, tgak = scr["tr"][bi], ("tr", bi)
                    tgb, tgbk = scr["ti"][bi], ("ti", bi)
                    act(tga[:, 0:w], pga[:, 0:w], AF.Tanh, [pgak, "dv"], [tgak], bias=dv[:, D_HBG + j:D_HBG + j + 1], scale=0.5)
                    act(tgb[:, 0:w], pgb[:, 0:w], AF.Tanh, [pgbk, "dv"], [tgbk], bias=dv[:, D_HBG + 8 + j:D_HBG + 8 + j + 1], scale=0.5)
                    stt(tga[:, 0:w], tga[:, 0:w], 1.0, pya[:, 0:w], ALU.add, ALU.mult, [tgak, pyak], [tgak])
                    stt(tgb[:, 0:w], tgb[:, 0:w], 1.0, pyb[:, 0:w], ALU.add, ALU.mult, [tgbk, pybk], [tgbk])
                    tt(mT[:, j, c0:c0 + w], tga[:, 0:w], tgb[:, 0:w], ALU.add, [tgak, tgbk], [("m", j, s)])
                blocks_done(4)

        kmA = [("mtA", c) for c in range(KC)]
        kmB = [("mtB", c) for c in range(KC)]
        kmH = [("mtH", c) for c in range(KC)]
        prologue_weights()
        run_boundary([], [dict(kind="IN", g=0, n=NMETA, src=meta_d, c0=T, xk=("xn", "m"))])
        issue_loads()
        subtiles = [(s, s * SW, SW) for s in range(NS)]
        seq_tiles = tiles
        run_boundary([], in_items(seq_tiles[0][1], seq_tiles[0][2]))
        pre_next = {}
        for ti, (_, b, n) in enumerate(seq_tiles):
            if n == 0 and ti > 0:
                cp(stA[:], mtA[:], kmA, kA)
                cp(stB[:], mtB[:], kmB, kB)
                cp(stH[:], mtH[:], kmH, kH)
            phase1(subtiles, meta_sub=(("m", T, NMETA) if ti == 0 else None), pre=(pre_next if ti > 0 else None))
            p3 = p3_items(b, n)
            nin = in_items(seq_tiles[ti + 1][1], seq_tiles[ti + 1][2]) if ti + 1 < len(seq_tiles) else []
            prefetch_boundary(p3, nin)
            phase2(subtiles)
            pre_next = {}
            run_boundary(p3, nin, tail_hook=((lambda d=pre_next: d.update(pre_project(subtiles))) if nin else None))
        P.op("sp", None, writes=[("xt", i) for i in range(NXT)])

        P.finalize()
        csem = {e: es.enter_context(nc.semaphore("c_" + e)) for e in Prog.ENGS}
        dsem = {k: es.enter_context(nc.semaphore("d_" + k)) for k in sorted(P.dma_cnt.keys())}
        with nc.Block() as block:
            @block.tensor
            def _(e):
                P.emit("pe", e, csem, dsem)

            @block.scalar
            def _(e):
                P.emit("act", e, csem, dsem)

            @block.vector
            def _(e):
                P.emit("dve", e, csem, dsem)

            @block.gpsimd
            def _(e):
                P.emit("pool", e, csem, dsem)

            @block.sync
            def _(e):
                P.emit("sp", e, csem, dsem)
    return nc


def _weights_map(inputs):
    f = lambda a: np.ascontiguousarray(np.asarray(a, dtype=np.float32))
    return {
        "meta": f(inputs["meta"]),
        "norm_g": f(inputs["norm_g"]).reshape(1, D),
        "w_in": f(inputs["w_in"]).reshape(D, 8 * D),
        "b_gate": f(inputs["b_gate"]).reshape(2, D),
        "conv_a_w": f(inputs["conv_a_w"]).reshape(3, D),
        "w_proj_a": f(inputs["w_proj_a"]).reshape(D, D),
        "conv_b_w": f(inputs["conv_b_w"]).reshape(4, D),
        "conv_b_b": f(inputs["conv_b_b"]).reshape(1, D),
        "w_rg_a": f(inputs["w_rg_a"]).reshape(4, 256, 256),
        "b_rg_a": f(inputs["b_rg_a"]).reshape(1, D),
        "w_rg_i": f(inputs["w_rg_i"]).reshape(4, 256, 256),
        "b_rg_i": f(inputs["b_rg_i"]).reshape(1, D),
        "lru_param": f(inputs["lru_param"]).reshape(1, D),
        "w_proj_b": f(inputs["w_proj_b"]).reshape(D, D),
        "w_out": f(inputs["w_out"]).reshape(D, D),
        "final_norm_g": f(inputs["final_norm_g"]).reshape(1, D),
    }


def run(inputs, n_cores, T=1024, trace=False):
    x = np.asarray(inputs["x"], dtype=np.float32)
    B, SEQ, _ = x.shape
    assert B % n_cores == 0
    NSEQ = B // n_cores
    nc = build_nc(NSEQ, SEQ, T)
    wm = _weights_map(inputs)
    in_maps = []
    for c in range(n_cores):
        m = dict(wm)
        m["x"] = np.ascontiguousarray(x[c * NSEQ:(c + 1) * NSEQ])
        in_maps.append(m)
    res = run_bass_kernel_spmd(nc, in_maps, core_ids=list(range(n_cores)), trace=trace)
    out = np.concatenate([np.asarray(r["out"]) for r in res.results], axis=0)
    return out.astype(np.float32), res


def kernel(**inputs):
    out, _ = run(inputs, N_CORES)
    return out
```

```python
import contextlib
import numpy as np
import concourse.bass as bass
import concourse.mybir as mybir
from concourse.bass_utils import run_bass_kernel_spmd

F32 = mybir.dt.float32
BF16 = mybir.dt.bfloat16
AF = mybir.ActivationFunctionType
ALU = mybir.AluOpType

D = 1024
KC = 8
SW = 512
NMETA = 16
EPS = 1e-6
N_CORES = 8


class _Op:
    __slots__ = ("eng", "fn", "pos", "dma", "dval", "waits", "targeted", "seq")


class Prog:
    ENGS = ("pe", "act", "dve", "pool", "sp")

    def __init__(self):
        self.by_eng = {e: [] for e in self.ENGS}
        self.last_w = {}
        self.readers = {}
        self.dma_cnt = {}
        self.waited_c = {}
        self.waited_d = {}

    def op(self, eng, fn, reads=(), writes=(), dma=None):
        o = _Op()
        o.eng, o.fn, o.dma = eng, fn, dma
        o.targeted, o.seq, o.dval = False, None, None
        lst = self.by_eng[eng]
        o.pos = len(lst)
        deps = {}
        for k in reads:
            w = self.last_w.get(k)
            if w is not None:
                deps[id(w)] = (w, True)
        for k in writes:
            w = self.last_w.get(k)
            if w is not None and id(w) not in deps:
                deps[id(w)] = (w, False)
            for r in self.readers.get(k, ()):
                if id(r) not in deps:
                    deps[id(r)] = (r, False)
        waits = []
        dmax = {}
        for d, raw in deps.values():
            if d.dma is not None and (d.dma not in dmax or dmax[d.dma].dval < d.dval):
                dmax[d.dma] = d
        for d in dmax.values():
            key = (eng, d.dma)
            if self.waited_d.get(key, 0) >= d.dval:
                continue
            self.waited_d[key] = d.dval
            waits.append(d)
        for d, raw in deps.values():
            if d is o or d.dma is not None:
                continue
            if d.eng == eng and dma is None:
                if eng == "pe" or not raw:
                    continue
            key = (eng, d.eng)
            if self.waited_c.get(key, -1) >= d.pos:
                continue
            self.waited_c[key] = d.pos
            d.targeted = True
            waits.append(d)
        o.waits = waits
        if dma is not None:
            self.dma_cnt[dma] = self.dma_cnt.get(dma, 0) + 16
            o.dval = self.dma_cnt[dma]
        for k in reads:
            self.readers.setdefault(k, []).append(o)
        for k in writes:
            self.last_w[k] = o
            self.readers[k] = []
        lst.append(o)
        return o

    def finalize(self):
        for e in self.ENGS:
            n = 0
            for o in self.by_eng[e]:
                if o.dma is None and o.targeted:
                    n += 1
                    o.seq = n

    def emit(self, eng, handle, csem, dsem):
        for o in self.by_eng[eng]:
            for d in o.waits:
                if d.dma is not None:
                    handle.wait_ge(dsem[d.dma], d.dval)
                else:
                    handle.wait_ge(csem[d.eng], d.seq)
            if o.fn is None:
                continue
            ins = o.fn(handle)
            if o.dma is not None:
                ins.then_inc(dsem[o.dma], 16)
            elif o.targeted:
                ins.then_inc(csem[o.eng], 1)


CAW, CBW, CBB, BRA, BRI, LAM, BG = 0, 24, 56, 64, 72, 80, 88
NV = 104
D_CAWH, D_HBRA, D_HBRI, D_HBG, D_C2, D_C4, D_TMP, D_ZERO = 0, 24, 32, 40, 56, 64, 72, 88
NDV = 92

NSLOT = 8
NXT = 5


def build_nc(NSEQ, SEQ, T):
    assert SEQ % T == 0 and T % SW == 0
    NS = T // SW
    NG = T // 128
    nc = bass.Bass("TRN2", target_bir_lowering=False)
    din = lambda name, shape: nc.dram_tensor(name, shape, F32, kind="ExternalInput").ap()
    x_d = din("x", [NSEQ, SEQ, D])
    meta_d = din("meta", [NMETA, D])
    norm_g_d = din("norm_g", [1, D])
    w_in_d = din("w_in", [D, 8 * D])
    b_gate_d = din("b_gate", [2, D])
    conv_a_w_d = din("conv_a_w", [3, D])
    w_proj_a_d = din("w_proj_a", [D, D])
    conv_b_w_d = din("conv_b_w", [4, D])
    conv_b_b_d = din("conv_b_b", [1, D])
    w_rg_a_d = din("w_rg_a", [4, 256, 256])
    b_rg_a_d = din("b_rg_a", [1, D])
    w_rg_i_d = din("w_rg_i", [4, 256, 256])
    b_rg_i_d = din("b_rg_i", [1, D])
    lru_d = din("lru_param", [1, D])
    w_proj_b_d = din("w_proj_b", [D, D])
    w_out_d = din("w_out", [D, D])
    fg_d = din("final_norm_g", [1, D])
    out_d = nc.dram_tensor("out", [NSEQ, SEQ, D], F32, kind="ExternalOutput").ap()
    wsc = nc.dram_tensor("wsc", [80, 128, D], BF16, kind="Internal").ap()

    P = Prog()
    es = contextlib.ExitStack()
    with es:
        sb = lambda name, shape, dt: es.enter_context(nc.sbuf_tensor(name, shape, dt))
        xnT = sb("xnT", [128, KC, T + NMETA], BF16)
        yaT = sb("yaT", [128, KC, T], BF16)
        ybT = sb("ybT", [128, KC, T], BF16)
        mT = sb("mT", [128, KC, T], BF16)
        wo = sb("wo", [128, KC, D], BF16)
        wrg = sb("wrg", [128, 2, 8, 256], BF16)
        gbc = sb("gbc", [128, D], F32)
        fgbc = sb("fgbc", [128, D], F32)
        slots = [sb(f"slot{i}", [128, KC, 128], BF16) for i in range(NSLOT)]
        xt = [sb(f"xt{i}", [128, D], F32) for i in range(NXT)]
        xs = [sb(f"xs{i}", [128, D], BF16) for i in range(3)]
        junk = sb("junk", [128, D], BF16)
        xcb = [sb(f"xcb{i}", [128, 2, SW], BF16) for i in range(2)]
        vecs = sb("vecs", [NV, 128], F32)
        vT = sb("vT", [128, NV], F32)
        dv = sb("dv", [128, NDV], F32)
        identf = sb("identf", [128, 128], F32)
        identb = sb("identb", [128, 128], BF16)
        mhalf = sb("mhalf", [128, 1], F32)
        stA = sb("stA", [128, KC, 2], F32)
        stB = sb("stB", [128, KC, 3], F32)
        stH = sb("stH", [128, KC, 1], F32)
        mtA = sb("mtA", [128, KC, 2], F32)
        mtB = sb("mtB", [128, KC, 3], F32)
        mtH = sb("mtH", [128, KC, 1], F32)
        sq = [sb(f"sq{i}", [128, 4], F32) for i in range(4)]
        SCW = SW + 4
        scr = {}

        def scratch(name, n):
            scr[name] = [sb(f"{name}{i}", [128, SCW], F32) for i in range(n)]

        scratch("xbuf", 4)
        scratch("xc", 4)
        scratch("tr", 2)
        scratch("ti", 2)
        scratch("aa", 2)
        scratch("mm", 4)
        scratch("tz", 4)
        scratch("tza", 2)
        scratch("ch", 2)
        scratch("cv", 2)
        scratch("atmp", 1)
        for nm in ("xbuf", "xc", "mm", "tz"):
            scr[nm] += [sb(f"{nm}M{i}", [128, NMETA + 4], F32) for i in range(2)]
        xcb.append(sb("xcbM", [128, 2, NMETA], BF16))
        ps = [es.enter_context(nc.psum_tensor(f"ps{i}", [128, SW], F32)) for i in range(8)]

        cnt = {"ps": 0, "slot": 0, "sq": 0, "xs": 0}

        def psum():
            i = cnt["ps"] % 8
            cnt["ps"] += 1
            return ps[i], ("ps", i)

        def dma(out, in_, reads, writes, key):
            P.op("sp", lambda e: e.dma_start(out=out, in_=in_), reads, writes, dma=key)

        def act(out, in_, func, reads, writes, bias=None, scale=None, accum_out=None):
            kw = {}
            if bias is not None:
                kw["bias"] = bias
            if scale is not None:
                kw["scale"] = scale
            if accum_out is not None:
                kw["accum_out"] = accum_out
            P.op("act", lambda e: e.activation(out=out, in_=in_, func=func, **kw), reads, writes)

        def tt(out, in0, in1, op, reads, writes, eng="dve"):
            P.op(eng, lambda e: e.tensor_tensor(out=out, in0=in0, in1=in1, op=op), reads, writes)

        def ts(out, in0, s1, s2, op0, op1, reads, writes, eng="dve"):
            if op1 is None:
                P.op(eng, lambda e: e.tensor_scalar(out=out, in0=in0, scalar1=s1, scalar2=None, op0=op0), reads, writes)
            else:
                P.op(eng, lambda e: e.tensor_scalar(out=out, in0=in0, scalar1=s1, scalar2=s2, op0=op0, op1=op1), reads, writes)

        def stt(out, in0, scalar, in1, op0, op1, reads, writes):
            P.op("dve", lambda e: e.scalar_tensor_tensor(out=out, in0=in0, scalar=scalar, in1=in1, op0=op0, op1=op1), reads, writes)

        def cp(out, in_, reads, writes, eng="dve"):
            P.op(eng, lambda e: e.tensor_copy(out=out, in_=in_), reads, writes)

        def mm(out, lhsT, rhs, start, stop, reads, writes):
            P.op("pe", lambda e: e.matmul(out, lhsT=lhsT, rhs=rhs, start=start, stop=stop), reads, writes)

        def tr_(out, in_, ident, reads, writes):
            P.op("pe", lambda e: e.transpose(out=out, in_=in_, identity=ident), reads, writes)

        ya_keys = [("ya", j, s) for j in range(KC) for s in range(NS)]
        yb_keys = [("yb", j, s) for j in range(KC) for s in range(NS)]
        m_keys = [("m", j, s) for j in range(KC) for s in range(NS)]

        P.op("pool", lambda e: e.memset(identf[:], 0.0), writes=["identf"])
        P.op("pool", lambda e: e.affine_select(out=identf[:], in_=identf[:], compare_op=ALU.not_equal, fill=1.0, base=0,
                                               pattern=[[-1, 128]], channel_multiplier=1), reads=["identf"], writes=["identf"])
        P.op("pool", lambda e: e.memset(mhalf[:], -0.5), writes=["mhalf"])
        kA = [("stA", c) for c in range(KC)]
        kB = [("stB", c) for c in range(KC)]
        kH = [("stH", c) for c in range(KC)]
        P.op("pool", lambda e: e.memset(stA[:], 0.0), writes=kA)
        P.op("pool", lambda e: e.memset(stB[:], 0.0), writes=kB)
        P.op("pool", lambda e: e.memset(stH[:], 0.0), writes=kH)
        cp(identb[:], identf[:], ["identf"], ["identb"])
        P.op("pool", lambda e: e.memset(dv[:, D_ZERO:D_ZERO + 1], 0.0), writes=["dvz"])
        vec_srcs = [(conv_a_w_d, CAW, 24), (conv_b_w_d, CBW, 32), (conv_b_b_d, CBB, 8), (b_rg_a_d, BRA, 8),
                    (b_rg_i_d, BRI, 8), (lru_d, LAM, 8), (b_gate_d, BG, 16)]
        for i, (src, r0, n) in enumerate(vec_srcs):
            dma(vecs[r0:r0 + n, :], src.rearrange("k (j p) -> (k j) p", p=128), [], [("vecs", i)], "cst")
        dma(gbc[:], norm_g_d.partition_broadcast(128), [], ["gbc"], "cstg")
        dma(fgbc[:], fg_d.partition_broadcast(128), [], ["fgbc"], "cstf")
        pst, pk = psum()
        tr_(pst[:, 0:NV], vecs[:], identf[0:NV, 0:NV], [("vecs", i) for i in range(7)] + ["identf"], [pk])
        cp(vT[:], pst[:, 0:NV], [pk], ["vT"])
        ts(dv[:, D_CAWH:D_CAWH + 24], vT[:, CAW:CAW + 24], 0.5, None, ALU.mult, None, ["vT"], ["dv"])
        ts(dv[:, D_HBRA:D_HBRA + 16], vT[:, BRA:BRA + 16], 0.5, None, ALU.mult, None, ["vT"], ["dv"])
        ts(dv[:, D_HBG:D_HBG + 16], vT[:, BG:BG + 16], 0.5, None, ALU.mult, None, ["vT"], ["dv"])
        act(dv[:, D_TMP:D_TMP + 8], vT[:, LAM:LAM + 8], AF.Exp, ["vT"], ["dv"], scale=-1.0)
        act(dv[:, D_TMP + 8:D_TMP + 16], dv[:, D_TMP:D_TMP + 8], AF.Ln, ["dv"], ["dv"], bias=1.0, scale=1.0)
        ts(dv[:, D_C2:D_C2 + 8], dv[:, D_TMP + 8:D_TMP + 16], -4.0, None, ALU.mult, None, ["dv"], ["dv"])
        ts(dv[:, D_C4:D_C4 + 8], dv[:, D_TMP + 8:D_TMP + 16], -8.0, None, ALU.mult, None, ["dv"], ["dv"])

        xtc = {"n": 0}

        def rstd_act(sqt, n, sqk):
            act(sqt[0:n, 1:2], sqt[0:n, 0:1], AF.Sqrt, [sqk], [sqk], bias=EPS, scale=1.0 / D)

        def rstd_dve(sqt, n, sqk):
            P.op("dve", lambda e: e.reciprocal(out=sqt[0:n, 2:3], in_=sqt[0:n, 1:2]), [sqk], [sqk])

        def item_load(it):
            bi = xtc["n"] % NXT
            xtc["n"] += 1
            it["xb"] = bi
            n = it["n"]
            dma(xt[bi][0:n, :], it["src"], [], [("xt", bi)], f"xl{bi}")

        def new_sq(it):
            sqi = cnt["sq"] % 4
            cnt["sq"] += 1
            it["sq"] = (sq[sqi], ("sq", sqi))

        def in_A_act(it):
            n, bi = it["n"], it["xb"]
            new_sq(it)
            sqt, sqk = it["sq"]
            act(junk[0:n, :], xt[bi][0:n, :], AF.Square, [("xt", bi)], [sqk], accum_out=sqt[0:n, 0:1])
            rstd_act(sqt, n, sqk)

        def in_A_pool(it):
            sqt, sqk = it["sq"]
            rstd_dve(sqt, it["n"], sqk)

        def in_B1_dve(it):
            n, bi = it["n"], it["xb"]
            sqt, sqk = it["sq"]
            jb = cnt["xs"] % 3
            cnt["xs"] += 1
            xsb, xsk = xs[jb], ("xs", jb)
            stt(xsb[0:n, :], xt[bi][0:n, :], sqt[0:n, 2:3], gbc[0:n, :], ALU.mult, ALU.mult, [("xt", bi), sqk, "gbc"], [xsk])
            it["xs"] = (xsb, xsk)

        def in_B1_pool(it):
            pass

        def in_B2(it):
            n = it["n"]
            xsb, xsk = it["xs"]
            pt, pk = psum()
            ptb = pt[:].bitcast(BF16)
            for kc in range(KC):
                tr_(ptb[:, kc * 128:kc * 128 + n], xsb[0:n, kc * 128:(kc + 1) * 128], identb[0:n, 0:n], [xsk, "identb"], [pk])
            it["pt"] = (ptb, pk)

        def in_C(it):
            n, g = it["n"], it["g"]
            ptb, pk = it["pt"]
            c0 = it.get("c0", g * 128)
            xk = it.get("xk", ("xn", (g * 128) // SW))
            act(xnT[:, :, c0:c0 + n], ptb.rearrange("p (k t) -> p k t", k=KC)[:, :, 0:n], AF.Copy, [pk], [xk])

        def p3_A_pe(it):
            g = it["g"]
            s_ = (g * 128) // SW
            it["po"] = []
            for half in range(2):
                po, pok = psum()
                for kc in range(KC):
                    mm(po[:], mT[:, kc, g * 128:(g + 1) * 128], wo[:, kc, half * SW:(half + 1) * SW], kc == 0, kc == KC - 1,
                       [("m", kc, s_), "wo"], [pok])
                it["po"].append((po, pok))

        def p3_A_dve(it, half):
            bi = it["xb"]
            xtb, xtk = xt[bi], ("xt", bi)
            if True:
                po, pok = it["po"][half]
                stt(xtb[:, half * SW:(half + 1) * SW], po[:], 0.5, xtb[:, half * SW:(half + 1) * SW], ALU.mult, ALU.add,
                    [pok, xtk], [xtk])

        def p3_B_act(it):
            bi = it["xb"]
            new_sq(it)
            sqt, sqk = it["sq"]
            act(junk[:], xt[bi][:], AF.Square, [("xt", bi)], [sqk], accum_out=sqt[:, 0:1])
            rstd_act(sqt, 128, sqk)

        def p3_B_pool(it):
            sqt, sqk = it["sq"]
            rstd_dve(sqt, 128, sqk)

        def p3_C(it):
            bi = it["xb"]
            xtb, xtk = xt[bi], ("xt", bi)
            sqt, sqk = it["sq"]
            stt(xtb[:], xtb[:], sqt[:, 2:3], fgbc[:], ALU.mult, ALU.mult, [xtk, sqk, "fgbc"], [xtk])
            dma(it["dst"], xtb[:], [xtk], [], f"st{bi}")

        def prefetch_boundary(p3, nin, k=4):
            items = []
            for i in range(max(len(p3), len(nin))):
                if i < len(p3):
                    items.append(p3[i])
                if i < len(nin):
                    items.append(nin[i])
            for it in items[:min(k, NXT - 1)]:
                item_load(it)
                it["loaded"] = True

        def run_boundary(p3, nin, tail_hook=None):
            n = max(len(p3), len(nin))
            loads = []
            for i in range(n):
                if i < len(p3):
                    loads.append(p3[i])
                if i < len(nin):
                    loads.append(nin[i])
            nl = [0]
            for it in loads:
                it["fin"] = False

            def load_more(maxahead):
                while nl[0] < len(loads) and nl[0] < maxahead:
                    k = nl[0]
                    if loads[k].get("loaded"):
                        nl[0] += 1
                        continue
                    if k >= NXT and not loads[k - NXT]["fin"]:
                        break
                    item_load(loads[k])
                    nl[0] += 1
            per = (1 if p3 else 0) + (1 if nin else 0)
            P_ = lambda k: p3[k] if 0 <= k < len(p3) else None
            I_ = lambda k: nin[k] if 0 <= k < len(nin) else None
            for t in range(n + 3):
                if t == n and tail_hook is not None:
                    tail_hook()
                load_more(per * (t + 2))
                if I_(t - 2):
                    in_B2(I_(t - 2))
                if P_(t):
                    p3_A_pe(P_(t))
                if P_(t - 1):
                    p3_B_act(P_(t - 1))
                if I_(t):
                    in_A_act(I_(t))
                if I_(t - 2):
                    in_C(I_(t - 2))
                if I_(t - 1):
                    in_B1_dve(I_(t - 1))
                    in_B1_pool(I_(t - 1))
                    I_(t - 1)["fin"] = True
                if P_(t - 1):
                    p3_B_pool(P_(t - 1))
                load_more(per * (t + 2) + 1)
                if P_(t):
                    p3_A_dve(P_(t), 0)
                if P_(t - 1):
                    p3_C(P_(t - 1))
                    P_(t - 1)["fin"] = True
                if I_(t):
                    in_A_pool(I_(t))
                if P_(t):
                    p3_A_dve(P_(t), 1)

        def in_items(b, n):
            return [dict(kind="IN", g=g, n=128, src=x_d[b, n * T + g * 128:n * T + (g + 1) * 128, :]) for g in range(NG)]

        def p3_items(b, n):
            return [dict(kind="P3", g=g, n=128, src=x_d[b, n * T + g * 128:n * T + (g + 1) * 128, :],
                         dst=out_d[b, n * T + g * 128:n * T + (g + 1) * 128, :]) for g in range(NG)]

        stg = [yaT[:].bitcast(F32), ybT[:].bitcast(F32), xnT[:].bitcast(F32)[:, :, 0:T // 2]]
        stg_keys = [ya_keys, yb_keys, [("xn", s) for s in range(NS)] + [("xn", "m")]]
        NSTG = 3
        mflat = mT[:].rearrange("p k t -> p (k t)")
        wb = [mflat[:, i * 4096:(i + 1) * 4096].rearrange("p (j k c) -> p j k c", j=4, k=KC) for i in range(2)]
        wb_keys = [[("m", j, s) for j in range(4 * i, 4 * i + 4) for s in range(NS)] for i in range(2)]
        def prologue_weights():
            rounds = []
            for g in (4, 5, 0, 1, 2, 3, 6, 7):
                for jh in range(2):
                    c0 = g * D + jh * 512
                    rounds.append(dict(src=w_in_d[:, c0:c0 + 512].rearrange("(kc p) n -> p kc n", p=128), blk0=g * 8 + jh * 4))
            for jh in range(2):
                rounds.append(dict(src=w_proj_a_d[:, jh * 512:(jh + 1) * 512].rearrange("(kc p) n -> p kc n", p=128), blk0=64 + jh * 4))
            for jh in range(2):
                rounds.append(dict(src=w_proj_b_d[:, jh * 512:(jh + 1) * 512].rearrange("(kc p) n -> p kc n", p=128), blk0=72 + jh * 4))
            for jh in range(2):
                rounds.append(dict(src=w_out_d[:, jh * 512:(jh + 1) * 512].rearrange("(kc p) n -> p kc n", p=128),
                                   sb=wo[:, :, jh * 512:(jh + 1) * 512], sbk=["wo"]))
            for gi, wsrc in enumerate((w_rg_a_d, w_rg_i_d)):
                rounds.append(dict(src=wsrc.rearrange("h (kc p) n -> p (h kc) n", p=128), sb=wrg[:, gi, :, :], sbk=["wrg"], ncol=256))

            def do_load(r):
                rd = rounds[r]
                i = r % NSTG
                dst = stg[i] if "ncol" not in rd else stg[i][:, :, 0:rd["ncol"]]
                dma(dst, rd["src"], [], stg_keys[i], f"pl{i}")

            nwb = [0]

            def do_cast_store(r):
                rd = rounds[r]
                i = r % NSTG
                if "blk0" in rd:
                    wi = nwb[0] % 2
                    nwb[0] += 1
                    o_ap = wb[wi].rearrange("p j k c -> p k j c")
                    i_ap = stg[i].rearrange("p k (j c) -> p k j c", j=4)
                    cp(o_ap, i_ap, stg_keys[i], wb_keys[wi])
                    b0 = rd["blk0"]
                    dma(wsc[b0:b0 + 4].rearrange("b p n -> p b n"), wb[wi].rearrange("p j k c -> p j (k c)"),
                        wb_keys[wi], [("wsc", b0 + q) for q in range(4)], f"pst{wi}")
                else:
                    src = stg[i] if "ncol" not in rd else stg[i][:, :, 0:rd["ncol"]]
                    act(rd["sb"], src, AF.Copy, stg_keys[i], rd["sbk"])

            for r in range(min(NSTG - 1, len(rounds))):
                do_load(r)
            for r in range(len(rounds)):
                if r + NSTG - 1 < len(rounds):
                    do_load(r + NSTG - 1)
                do_cast_store(r)

        tiles = [("seq", b, n) for b in range(NSEQ) for n in range(SEQ // T)]
        seq_blocks = []
        for t in tiles:
            for h in range(4):
                for g in (4, 5):
                    for c in (2 * h, 2 * h + 1):
                        seq_blocks.append(g * 8 + c)
                for c in (2 * h, 2 * h + 1):
                    for g in (0, 1, 2, 3):
                        seq_blocks.append(g * 8 + c)
            if t[0] != "meta":
                for j in range(KC):
                    seq_blocks += [48 + j, 56 + j, 64 + j, 72 + j]
        bs = {"issued": 0, "used": 0, "done": 0}

        def issue_loads():
            upto = min(bs["done"] + NSLOT, len(seq_blocks))
            while bs["issued"] < upto:
                k = bs["issued"]
                blk = seq_blocks[k]
                si = k % NSLOT
                dma(slots[si][:].rearrange("p k c -> p (k c)"), wsc[blk], [("wsc", blk)], [("slot", si)], f"wl{si}")
                bs["issued"] += 1

        def next_block(blk):
            k = bs["used"]
            assert seq_blocks[k] == blk, (k, seq_blocks[k], blk)
            assert k < bs["issued"], (k, bs["issued"])
            bs["used"] += 1
            si = k % NSLOT
            return slots[si], ("slot", si)

        def blocks_done(n):
            bs["done"] += n
            assert bs["done"] == bs["used"]
            issue_loads()

        def proj_group(slot, skey, s, c0, w):
            pt, pk = psum()
            for kc in range(KC):
                mm(pt[:, 0:w], slot[:, kc, :], xnT[:, kc, c0:c0 + w], kc == 0, kc == KC - 1, [skey, ("xn", s)], [pk])
            return pt, pk

        unit = [0]
        zcol = dv[:, D_ZERO:D_ZERO + 1]

        def interleave(*lists):
            out = []
            n = max(len(l) for l in lists)
            for i in range(n):
                for l in lists:
                    if i < len(l):
                        out.append(l[i])
            for f in out:
                f()

        def pre_project(subtiles):
            cs = (0, 1)
            sl_xb = [next_block(4 * 8 + c) for c in cs]
            sl_zb = [next_block(5 * 8 + c) for c in cs]
            s, c0, w = subtiles[0]
            pz = [proj_group(sl_zb[ci][0], sl_zb[ci][1], s, c0, w) for ci in range(2)]
            pt = [proj_group(sl_xb[ci][0], sl_xb[ci][1], s, c0, w) for ci in range(2)]
            return dict(sl_xb=sl_xb, sl_zb=sl_zb, pz=pz, pt=pt, s=s)

        def phase1(subtiles, meta_sub=None, pre=None):
            for h in range(4):
                cs = (2 * h, 2 * h + 1)
                if h == 0 and pre:
                    sl_xb, sl_zb = pre["sl_xb"], pre["sl_zb"]
                else:
                    sl_xb = [next_block(4 * 8 + c) for c in cs]
                    sl_zb = [next_block(5 * 8 + c) for c in cs]
                bst = []

                def b_s123(s, c0, w, is_meta):
                    if is_meta:
                        u = 2
                    else:
                        u = unit[0] % 2
                        unit[0] += 1
                    st = dict(u=u, s=s, c0=c0, w=w, meta=is_meta)
                    xcbt, xcbk = xcb[u], ("xcb", u)
                    for ci, c in enumerate(cs):
                        if is_meta:
                            break
                        bi = u * 2 + ci
                        if h == 0 and pre and pre["s"] == s and not is_meta:
                            pz, pzk = pre["pz"][ci]
                        else:
                            pz, pzk = proj_group(sl_zb[ci][0], sl_zb[ci][1], s, c0, w)
                        act(scr["tz"][bi][:, 0:w], pz[:, 0:w], AF.Tanh, [pzk], [("tz", bi)], scale=0.5)
                        stt(scr["tz"][bi][:, 0:w], scr["tz"][bi][:, 0:w], 1.0, pz[:, 0:w], ALU.add, ALU.mult,
                            [("tz", bi), pzk], [("tz", bi)])
                    for ci, c in enumerate(cs):
                        bi = u * 2 + ci
                        xb_t, xb_k = scr["xbuf"][bi], ("xbuf", bi)
                        xc_t, xc_k = scr["xc"][bi], ("xc", bi)
                        if h == 0 and pre and pre["s"] == s and not is_meta:
                            pt, pk = pre["pt"][ci]
                        else:
                            pt, pk = proj_group(sl_xb[ci][0], sl_xb[ci][1], s, c0, w)
                        act(xb_t[:, 0:3], stB[:, c, :], AF.Copy, [("stB", c)], [xb_k])
                        act(xb_t[:, 3:3 + w], pt[:, 0:w], AF.Copy, [pk], [xb_k])
                        act(stB[:, c, :], xb_t[:, w:w + 3], AF.Copy, [xb_k], [("stB", c)])
                        ts(xc_t[:, 0:w], xb_t[:, 0:w], vT[:, CBW + c:CBW + c + 1], vT[:, CBB + c:CBB + c + 1],
                           ALU.mult, ALU.add, [xb_k, "vT"], [xc_k])
                        for k in range(1, 4):
                            stt(xc_t[:, 0:w], xb_t[:, k:k + w], vT[:, CBW + 8 * k + c:CBW + 8 * k + c + 1], xc_t[:, 0:w],
                                ALU.mult, ALU.add, [xb_k, xc_k, "vT"], [xc_k])
                        cp(xcbt[:, ci, 0:w], xc_t[:, 0:w], [xc_k], [xcbk])
                    return st

                def b_gates_pe(st):
                    u, s, c0, w = st["u"], st["s"], st["c0"], st["w"]
                    xcbt, xcbk = xcb[u], ("xcb", u)
                    gp = []
                    for ci, c in enumerate(cs):
                        pr, prk = psum()
                        for kc in range(2):
                            mm(pr[:, 0:w], wrg[:, 0, h * 2 + kc, ci * 128:(ci + 1) * 128], xcbt[:, kc, 0:w], kc == 0, kc == 1,
                               ["wrg", xcbk], [prk])
                        pi, pik = psum()
                        for kc in range(2):
                            mm(pi[:, 0:w], wrg[:, 1, h * 2 + kc, ci * 128:(ci + 1) * 128], xcbt[:, kc, 0:w], kc == 0, kc == 1,
                               ["wrg", xcbk], [pik])
                        gp.append((pr, prk, pi, pik))
                    st["gp"] = gp

                def b_gates_act(st, ci):
                    u, s, c0, w = st["u"], st["s"], st["c0"], st["w"]
                    c = cs[ci]
                    bi = u * 2 + ci
                    pr, prk, pi, pik = st["gp"][ci]
                    act(scr["tr"][ci][:, 0:w], pr[:, 0:w], AF.Tanh, [prk, "dv"], [("tr", ci)],
                        bias=dv[:, D_HBRA + c:D_HBRA + c + 1], scale=0.5)
                    act(scr["ti"][ci][:, 0:w], pi[:, 0:w], AF.Tanh, [pik, "dv"], [("ti", ci)],
                        bias=dv[:, D_HBRI + c:D_HBRI + c + 1], scale=0.5)
                    act(scr["aa"][ci][:, 0:w], scr["tr"][ci][:, 0:w], AF.Exp, [("tr", ci), "dv"], [("aa", ci)],
                        bias=dv[:, D_C2 + c:D_C2 + c + 1], scale=dv[:, D_C2 + c:D_C2 + c + 1])
                    act(scr["mm"][bi][:, 0:w], scr["tr"][ci][:, 0:w], AF.Exp, [("tr", ci), "dv"], [("mm", bi)],
                        bias=dv[:, D_C4 + c:D_C4 + c + 1], scale=dv[:, D_C4 + c:D_C4 + c + 1])
                    act(scr["mm"][bi][:, 0:w], scr["mm"][bi][:, 0:w], AF.Relu, [("mm", bi)], [("mm", bi)],
                        bias=1.0 / 16, scale=-1.0 / 16)

                def b_sqrt_act(st):
                    u, w = st["u"], st["w"]
                    for ci, c in enumerate(cs):
                        bi = u * 2 + ci
                        act(scr["mm"][bi][:, 0:w], scr["mm"][bi][:, 0:w], AF.Sqrt, [("mm", bi)], [("mm", bi)])

                def b_tail_ops(st):
                    u, s, c0, w = st["u"], st["s"], st["c0"], st["w"]
                    is_meta = st["meta"]
                    ops = []
                    for ci, c in enumerate(cs):
                        bi = u * 2 + ci
                        hh, hk = scr["xc"][bi], ("xc", bi)
                        ti_t, ti_k = scr["ti"][ci], ("ti", ci)
                        aa_t, aa_k = scr["aa"][ci], ("aa", ci)
                        ops.append(lambda bi=bi, ti_t=ti_t, ti_k=ti_k: stt(ti_t[:, 0:w], ti_t[:, 0:w], 1.0, scr["xc"][bi][:, 0:w],
                                                                           ALU.add, ALU.mult, [ti_k, ("xc", bi)], [ti_k]))
                        if is_meta:
                            ops.append(lambda bi=bi: P.op("dve", lambda e, t=scr["mm"][bi]: e.memset(t[:, 0:1], 0.25), [], [("mm", bi)]))
                        ops.append(lambda bi=bi, ti_t=ti_t, ti_k=ti_k: tt(ti_t[:, 0:w], scr["mm"][bi][:, 0:w], ti_t[:, 0:w], ALU.mult,
                                                                          [("mm", bi), ti_k], [ti_k]))
                        ops.append(lambda c=c, hh=hh, hk=hk, ti_t=ti_t, ti_k=ti_k, aa_t=aa_t, aa_k=aa_k: P.op(
                            "dve", lambda e, o=hh[:, 0:w], a=aa_t[:, 0:w], uu=ti_t[:, 0:w], ini=stH[:, c, :]:
                            e.tensor_tensor_scan(out=o, data0=a, data1=uu, initial=ini, op0=ALU.mult, op1=ALU.add),
                            [aa_k, ti_k, ("stH", c)], [hk]))
                        ops.append(lambda c=c, hh=hh, hk=hk: cp(stH[:, c, :], hh[:, w - 1:w], [hk], [("stH", c)]))
                        if not is_meta:
                            ops.append(lambda bi=bi, c=c, hh=hh, hk=hk: tt(ybT[:, c, c0:c0 + w], hh[:, 0:w], scr["tz"][bi][:, 0:w], ALU.mult,
                                                                           [hk, ("tz", bi)], [("yb", c, s)], eng="pool"))
                    return ops

                def a_pe(c, sl, s, c0, w):
                    ai = unit[0] % 2
                    unit[0] += 1
                    ph, phk = proj_group(sl[2][0], sl[2][1], s, c0, w)
                    pz, pzk = proj_group(sl[3][0], sl[3][1], s, c0, w)
                    pc, pck = proj_group(sl[1][0], sl[1][1], s, c0, w)
                    pb, pbk = proj_group(sl[0][0], sl[0][1], s, c0, w)
                    return dict(ai=ai, c=c, s=s, c0=c0, w=w, pb=(pb, pbk), pc=(pc, pck), ph=(ph, phk), pz=(pz, pzk))

                def a_act(au):
                    ai, w = au["ai"], au["w"]
                    act(scr["cv"][ai][:, 0:w], au["ph"][0][:, 0:w], AF.Copy, [au["ph"][1]], [("cv", ai)])
                    act(scr["tza"][ai][:, 0:w], au["pz"][0][:, 0:w], AF.Tanh, [au["pz"][1]], [("tza", ai)], scale=0.5)

                def a_dve_ops(au, is_meta=False):
                    ai, c, s, c0, w = au["ai"], au["c"], au["s"], au["c0"], au["w"]
                    pb, pbk = au["pb"]
                    pc, pck = au["pc"]
                    pz, pzk = au["pz"]
                    tz_t, tz_k = scr["tza"][ai], ("tza", ai)
                    ch_t, ch_k = scr["ch"][ai], ("ch", ai)
                    cv_t, cv_k = scr["cv"][ai], ("cv", ai)
                    ops = []
                    ops.append(lambda: cp(ch_t[:, 0:2], stA[:, c, :], [("stA", c)], [ch_k]))
                    ops.append(lambda: tt(ch_t[:, 2:2 + w], pc[:, 0:w], cv_t[:, 0:w], ALU.mult, [pck, cv_k], [ch_k]))
                    ops.append(lambda: stt(tz_t[:, 0:w], tz_t[:, 0:w], 1.0, pz[:, 0:w], ALU.add, ALU.mult, [tz_k, pzk], [tz_k]))
                    ops.append(lambda: cp(stA[:, c, :], ch_t[:, w:w + 2], [ch_k], [("stA", c)]))
                    if is_meta:
                        return [ops[0], ops[1], ops[3]]
                    ops.append(lambda: tt(tz_t[:, 0:w], tz_t[:, 0:w], pb[:, 0:w], ALU.mult, [tz_k, pbk], [tz_k]))

                    def pool_part():
                        at_t, at_k = scr["atmp"][0], ("atmp", 0)
                        ts(cv_t[:, 0:w], ch_t[:, 0:w], dv[:, D_CAWH + c:D_CAWH + c + 1], zcol, ALU.mult, ALU.add,
                           [ch_k, "dv", "dvz"], [cv_k], eng="pool")
                        for k in range(1, 3):
                            ts(at_t[:, 0:w], ch_t[:, k:k + w], dv[:, D_CAWH + 8 * k + c:D_CAWH + 8 * k + c + 1], zcol, ALU.mult, ALU.add,
                               [ch_k, "dv", "dvz"], [at_k], eng="pool")
                            tt(cv_t[:, 0:w], cv_t[:, 0:w], at_t[:, 0:w], ALU.add, [cv_k, at_k], [cv_k], eng="pool")
                        tt(yaT[:, c, c0:c0 + w], cv_t[:, 0:w], tz_t[:, 0:w], ALU.mult, [cv_k, tz_k], [("ya", c, s)], eng="pool")
                    ops.append(pool_part)
                    return ops

                if meta_sub is not None:
                    bst.append(b_s123(meta_sub[0], meta_sub[1], meta_sub[2], True))
                    for c in cs:
                        act(mtB[:, c, :], stB[:, c, :], AF.Copy, [("stB", c)], [("mtB", c)])
                for (s, c0, w) in subtiles:
                    bst.append(b_s123(s, c0, w, False))
                blocks_done(4)

                a_units = [(c, si) for c in cs for si in range(len(subtiles))]
                sl = None
                for idx, (c, si) in enumerate(a_units):
                    if si == 0:
                        sl = [next_block(g * 8 + c) for g in range(4)]
                        if meta_sub is not None:
                            mau = a_pe(c, sl, meta_sub[0], meta_sub[1], meta_sub[2])
                            a_act(mau)
                            interleave(a_dve_ops(mau, True))
                            cp(mtA[:, c, :], stA[:, c, :], [("stA", c)], [("mtA", c)])
                    s, c0, w = subtiles[si]
                    bq = idx - (len(a_units) - len(bst))
                    has_b = 0 <= bq < len(bst)
                    if has_b:
                        b_gates_pe(bst[bq])
                    au = a_pe(c, sl, s, c0, w)
                    if has_b:
                        b_gates_act(bst[bq], 0)
                        a_act(au)
                        b_gates_act(bst[bq], 1)
                        b_sqrt_act(bst[bq])
                        interleave(a_dve_ops(au))
                        interleave(b_tail_ops(bst[bq]))
                        if bst[bq]["meta"]:
                            for c_ in cs:
                                cp(mtH[:, c_, :], stH[:, c_, :], [("stH", c_)], [("mtH", c_)])
                    else:
                        a_act(au)
                        interleave(a_dve_ops(au))
                    if si == len(subtiles) - 1:
                        blocks_done(4)

        def phase2(subtiles):
            for j in range(KC):
                sga = next_block(48 + j)
                sgb = next_block(56 + j)
                spa = next_block(64 + j)
                spb = next_block(72 + j)
                for (s, c0, w) in subtiles:
                    bi = unit[0] % 2
                    unit[0] += 1
                    pga, pgak = proj_group(sga[0], sga[1], s, c0, w)
                    pgb, pgbk = proj_group(sgb[0], sgb[1], s, c0, w)
                    pya, pyak = psum()
                    for kc in range(KC):
                        mm(pya[:, 0:w], spa[0][:, kc, :], yaT[:, kc, c0:c0 + w], kc == 0, kc == KC - 1,
                           [spa[1], ("ya", kc, s)], [pyak])
                    pyb, pybk = psum()
                    for kc in range(KC):
                        mm(pyb[:, 0:w], spb[0][:, kc, :], ybT[:, kc, c0:c0 + w], kc == 0, kc == KC - 1,
                           [spb[1], ("yb", kc, s)], [pybk])
                    tga, tgak = scr["tr"][bi], ("tr", bi)
                    tgb, tgbk = scr["ti"][bi], ("ti", bi)
                    act(tga[:, 0:w], pga[:, 0:w], AF.Tanh, [pgak, "dv"], [tgak], bias=dv[:, D_HBG + j:D_HBG + j + 1], scale=0.5)
                    act(tgb[:, 0:w], pgb[:, 0:w], AF.Tanh, [pgbk, "dv"], [tgbk], bias=dv[:, D_HBG + 8 + j:D_HBG + 8 + j + 1], scale=0.5)
                    stt(tga[:, 0:w], tga[:, 0:w], 1.0, pya[:, 0:w], ALU.add, ALU.mult, [tgak, pyak], [tgak])
                    stt(tgb[:, 0:w], tgb[:, 0:w], 1.0, pyb[:, 0:w], ALU.add, ALU.mult, [tgbk, pybk], [tgbk])
                    tt(mT[:, j, c0:c0 + w], tga[:, 0:w], tgb[:, 0:w], ALU.add, [tgak, tgbk], [("m", j, s)])
                blocks_done(4)

        kmA = [("mtA", c) for c in range(KC)]
        kmB = [("mtB", c) for c in range(KC)]
        kmH = [("mtH", c) for c in range(KC)]
        prologue_weights()
        run_boundary([], [dict(kind="IN", g=0, n=NMETA, src=meta_d, c0=T, xk=("xn", "m"))])
        issue_loads()
        subtiles = [(s, s * SW, SW) for s in range(NS)]
        seq_tiles = tiles
        run_boundary([], in_items(seq_tiles[0][1], seq_tiles[0][2]))
        pre_next = {}
        for ti, (_, b, n) in enumerate(seq_tiles):
            if n == 0 and ti > 0:
                cp(stA[:], mtA[:], kmA, kA)
                cp(stB[:], mtB[:], kmB, kB)
                cp(stH[:], mtH[:], kmH, kH)
            phase1(subtiles, meta_sub=(("m", T, NMETA) if ti == 0 else None), pre=(pre_next if ti > 0 else None))
            p3 = p3_items(b, n)
            nin = in_items(seq_tiles[ti + 1][1], seq_tiles[ti + 1][2]) if ti + 1 < len(seq_tiles) else []
            prefetch_boundary(p3, nin)
            phase2(subtiles)
            pre_next = {}
            run_boundary(p3, nin, tail_hook=((lambda d=pre_next: d.update(pre_project(subtiles))) if nin else None))
        P.op("sp", None, writes=[("xt", i) for i in range(NXT)])

        P.finalize()
        csem = {e: es.enter_context(nc.semaphore("c_" + e)) for e in Prog.ENGS}
        dsem = {k: es.enter_context(nc.semaphore("d_" + k)) for k in sorted(P.dma_cnt.keys())}
        with nc.Block() as block:
            @block.tensor
            def _(e):
                P.emit("pe", e, csem, dsem)

            @block.scalar
            def _(e):
                P.emit("act", e, csem, dsem)

            @block.vector
            def _(e):
                P.emit("dve", e, csem, dsem)

            @block.gpsimd
            def _(e):
                P.emit("pool", e, csem, dsem)

            @block.sync
            def _(e):
                P.emit("sp", e, csem, dsem)
    return nc


def _weights_map(inputs):
    f = lambda a: np.ascontiguousarray(np.asarray(a, dtype=np.float32))
    return {
        "meta": f(inputs["meta"]),
        "norm_g": f(inputs["norm_g"]).reshape(1, D),
        "w_in": f(inputs["w_in"]).reshape(D, 8 * D),
        "b_gate": f(inputs["b_gate"]).reshape(2, D),
        "conv_a_w": f(inputs["conv_a_w"]).reshape(3, D),
        "w_proj_a": f(inputs["w_proj_a"]).reshape(D, D),
        "conv_b_w": f(inputs["conv_b_w"]).reshape(4, D),
        "conv_b_b": f(inputs["conv_b_b"]).reshape(1, D),
        "w_rg_a": f(inputs["w_rg_a"]).reshape(4, 256, 256),
        "b_rg_a": f(inputs["b_rg_a"]).reshape(1, D),
        "w_rg_i": f(inputs["w_rg_i"]).reshape(4, 256, 256),
        "b_rg_i": f(inputs["b_rg_i"]).reshape(1, D),
        "lru_param": f(inputs["lru_param"]).reshape(1, D),
        "w_proj_b": f(inputs["w_proj_b"]).reshape(D, D),
        "w_out": f(inputs["w_out"]).reshape(D, D),
        "final_norm_g": f(inputs["final_norm_g"]).reshape(1, D),
    }


def run(inputs, n_cores, T=1024, trace=False):
    x = np.asarray(inputs["x"], dtype=np.float32)
    B, SEQ, _ = x.shape
    assert B % n_cores == 0
    NSEQ = B // n_cores
    nc = build_nc(NSEQ, SEQ, T)
    wm = _weights_map(inputs)
    in_maps = []
    for c in range(n_cores):
        m = dict(wm)
        m["x"] = np.ascontiguousarray(x[c * NSEQ:(c + 1) * NSEQ])
        in_maps.append(m)
    res = run_bass_kernel_spmd(nc, in_maps, core_ids=list(range(n_cores)), trace=trace)
    out = np.concatenate([np.asarray(r["out"]) for r in res.results], axis=0)
    return out.astype(np.float32), res


def kernel(**inputs):
    out, _ = run(inputs, N_CORES)
    return out
```

```python
import contextlib
import numpy as np
import concourse.bass as bass
import concourse.mybir as mybir
from concourse.bass_utils import run_bass_kernel_spmd

F32 = mybir.dt.float32
BF16 = mybir.dt.bfloat16
AF = mybir.ActivationFunctionType
ALU = mybir.AluOpType

D = 1024
KC = 8
SW = 512
NMETA = 16
EPS = 1e-6
N_CORES = 8


class _Op:
    __slots__ = ("eng", "fn", "pos", "dma", "dval", "waits", "targeted", "seq")


class Prog:
    ENGS = ("pe", "act", "dve", "pool", "sp")

    def __init__(self):
        self.by_eng = {e: [] for e in self.ENGS}
        self.last_w = {}
        self.readers = {}
        self.dma_cnt = {}
        self.waited_c = {}
        self.waited_d = {}

    def op(self, eng, fn, reads=(), writes=(), dma=None):
        o = _Op()
        o.eng, o.fn, o.dma = eng, fn, dma
        o.targeted, o.seq, o.dval = False, None, None
        lst = self.by_eng[eng]
        o.pos = len(lst)
        deps = {}
        for k in reads:
            w = self.last_w.get(k)
            if w is not None:
                deps[id(w)] = (w, True)
        for k in writes:
            w = self.last_w.get(k)
            if w is not None and id(w) not in deps:
                deps[id(w)] = (w, False)
            for r in self.readers.get(k, ()):
                if id(r) not in deps:
                    deps[id(r)] = (r, False)
        waits = []
        dmax = {}
        for d, raw in deps.values():
            if d.dma is not None and (d.dma not in dmax or dmax[d.dma].dval < d.dval):
                dmax[d.dma] = d
        for d in dmax.values():
            key = (eng, d.dma)
            if self.waited_d.get(key, 0) >= d.dval:
                continue
            self.waited_d[key] = d.dval
            waits.append(d)
        for d, raw in deps.values():
            if d is o or d.dma is not None:
                continue
            if d.eng == eng and dma is None:
                if eng == "pe" or not raw:
                    continue
            key = (eng, d.eng)
            if self.waited_c.get(key, -1) >= d.pos:
                continue
            self.waited_c[key] = d.pos
            d.targeted = True
            waits.append(d)
        o.waits = waits
        if dma is not None:
            self.dma_cnt[dma] = self.dma_cnt.get(dma, 0) + 16
            o.dval = self.dma_cnt[dma]
        for k in reads:
            self.readers.setdefault(k, []).append(o)
        for k in writes:
            self.last_w[k] = o
            self.readers[k] = []
        lst.append(o)
        return o

    def finalize(self):
        for e in self.ENGS:
            n = 0
            for o in self.by_eng[e]:
                if o.dma is None and o.targeted:
                    n += 1
                    o.seq = n

    def emit(self, eng, handle, csem, dsem):
        for o in self.by_eng[eng]:
            for d in o.waits:
                if d.dma is not None:
                    handle.wait_ge(dsem[d.dma], d.dval)
                else:
                    handle.wait_ge(csem[d.eng], d.seq)
            if o.fn is None:
                continue
            ins = o.fn(handle)
            if o.dma is not None:
                ins.then_inc(dsem[o.dma], 16)
            elif o.targeted:
                ins.then_inc(csem[o.eng], 1)


CAW, CBW, CBB, BRA, BRI, LAM, BG = 0, 24, 56, 64, 72, 80, 88
NV = 104
D_CAWH, D_HBRA, D_HBRI, D_HBG, D_C2, D_C4, D_TMP, D_ZERO = 0, 24, 32, 40, 56, 64, 72, 88
NDV = 92

NSLOT = 8
NXT = 5


def build_nc(NSEQ, SEQ, T):
    assert SEQ % T == 0 and T % SW == 0
    NS = T // SW
    NG = T // 128
    nc = bass.Bass("TRN2", target_bir_lowering=False)
    din = lambda name, shape: nc.dram_tensor(name, shape, F32, kind="ExternalInput").ap()
    x_d = din("x", [NSEQ, SEQ, D])
    meta_d = din("meta", [NMETA, D])
    norm_g_d = din("norm_g", [1, D])
    w_in_d = din("w_in", [D, 8 * D])
    b_gate_d = din("b_gate", [2, D])
    conv_a_w_d = din("conv_a_w", [3, D])
    w_proj_a_d = din("w_proj_a", [D, D])
    conv_b_w_d = din("conv_b_w", [4, D])
    conv_b_b_d = din("conv_b_b", [1, D])
    w_rg_a_d = din("w_rg_a", [4, 256, 256])
    b_rg_a_d = din("b_rg_a", [1, D])
    w_rg_i_d = din("w_rg_i", [4, 256, 256])
    b_rg_i_d = din("b_rg_i", [1, D])
    lru_d = din("lru_param", [1, D])
    w_proj_b_d = din("w_proj_b", [D, D])
    w_out_d = din("w_out", [D, D])
    fg_d = din("final_norm_g", [1, D])
    out_d = nc.dram_tensor("out", [NSEQ, SEQ, D], F32, kind="ExternalOutput").ap()
    wsc = nc.dram_tensor("wsc", [80, 128, D], BF16, kind="Internal").ap()

    P = Prog()
    es = contextlib.ExitStack()
    with es:
        sb = lambda name, shape, dt: es.enter_context(nc.sbuf_tensor(name, shape, dt))
        xnT = sb("xnT", [128, KC, T + NMETA], BF16)
        yaT = sb("yaT", [128, KC, T], BF16)
        ybT = sb("ybT", [128, KC, T], BF16)
        mT = sb("mT", [128, KC, T], BF16)
        wo = sb("wo", [128, KC, D], BF16)
        wrg = sb("wrg", [128, 2, 8, 256], BF16)
        gbc = sb("gbc", [128, D], F32)
        fgbc = sb("fgbc", [128, D], F32)
        slots = [sb(f"slot{i}", [128, KC, 128], BF16) for i in range(NSLOT)]
        xt = [sb(f"xt{i}", [128, D], F32) for i in range(NXT)]
        xs = [sb(f"xs{i}", [128, D], BF16) for i in range(3)]
        junk = sb("junk", [128, D], BF16)
        xcb = [sb(f"xcb{i}", [128, 2, SW], BF16) for i in range(2)]
        vecs = sb("vecs", [NV, 128], F32)
        vT = sb("vT", [128, NV], F32)
        dv = sb("dv", [128, NDV], F32)
        identf = sb("identf", [128, 128], F32)
        identb = sb("identb", [128, 128], BF16)
        mhalf = sb("mhalf", [128, 1], F32)
        stA = sb("stA", [128, KC, 2], F32)
        stB = sb("stB", [128, KC, 3], F32)
        stH = sb("stH", [128, KC, 1], F32)
        mtA = sb("mtA", [128, KC, 2], F32)
        mtB = sb("mtB", [128, KC, 3], F32)
        mtH = sb("mtH", [128, KC, 1], F32)
        sq = [sb(f"sq{i}", [128, 4], F32) for i in range(4)]
        SCW = SW + 4
        scr = {}

        def scratch(name, n):
            scr[name] = [sb(f"{name}{i}", [128, SCW], F32) for i in range(n)]

        scratch("xbuf", 4)
        scratch("xc", 4)
        scratch("tr", 2)
        scratch("ti", 2)
        scratch("aa", 2)
        scratch("mm", 4)
        scratch("tz", 4)
        scratch("tza", 2)
        scratch("ch", 2)
        scratch("cv", 2)
        scratch("atmp", 1)
        for nm in ("xbuf", "xc", "mm", "tz"):
            scr[nm] += [sb(f"{nm}M{i}", [128, NMETA + 4], F32) for i in range(2)]
        xcb.append(sb("xcbM", [128, 2, NMETA], BF16))
        ps = [es.enter_context(nc.psum_tensor(f"ps{i}", [128, SW], F32)) for i in range(8)]

        cnt = {"ps": 0, "slot": 0, "sq": 0, "xs": 0}

        def psum():
            i = cnt["ps"] % 8
            cnt["ps"] += 1
            return ps[i], ("ps", i)

        def dma(out, in_, reads, writes, key):
            P.op("sp", lambda e: e.dma_start(out=out, in_=in_), reads, writes, dma=key)

        def act(out, in_, func, reads, writes, bias=None, scale=None, accum_out=None):
            kw = {}
            if bias is not None:
                kw["bias"] = bias
            if scale is not None:
                kw["scale"] = scale
            if accum_out is not None:
                kw["accum_out"] = accum_out
            P.op("act", lambda e: e.activation(out=out, in_=in_, func=func, **kw), reads, writes)

        def tt(out, in0, in1, op, reads, writes, eng="dve"):
            P.op(eng, lambda e: e.tensor_tensor(out=out, in0=in0, in1=in1, op=op), reads, writes)

        def ts(out, in0, s1, s2, op0, op1, reads, writes, eng="dve"):
            if op1 is None:
                P.op(eng, lambda e: e.tensor_scalar(out=out, in0=in0, scalar1=s1, scalar2=None, op0=op0), reads, writes)
            else:
                P.op(eng, lambda e: e.tensor_scalar(out=out, in0=in0, scalar1=s1, scalar2=s2, op0=op0, op1=op1), reads, writes)

        def stt(out, in0, scalar, in1, op0, op1, reads, writes):
            P.op("dve", lambda e: e.scalar_tensor_tensor(out=out, in0=in0, scalar=scalar, in1=in1, op0=op0, op1=op1), reads, writes)

        def cp(out, in_, reads, writes, eng="dve"):
            P.op(eng, lambda e: e.tensor_copy(out=out, in_=in_), reads, writes)

        def mm(out, lhsT, rhs, start, stop, reads, writes):
            P.op("pe", lambda e: e.matmul(out, lhsT=lhsT, rhs=rhs, start=start, stop=stop), reads, writes)

        def tr_(out, in_, ident, reads, writes):
            P.op("pe", lambda e: e.transpose(out=out, in_=in_, identity=ident), reads, writes)

        ya_keys = [("ya", j, s) for j in range(KC) for s in range(NS)]
        yb_keys = [("yb", j, s) for j in range(KC) for s in range(NS)]
        m_keys = [("m", j, s) for j in range(KC) for s in range(NS)]

        P.op("pool", lambda e: e.memset(identf[:], 0.0), writes=["identf"])
        P.op("pool", lambda e: e.affine_select(out=identf[:], in_=identf[:], compare_op=ALU.not_equal, fill=1.0, base=0,
                                               pattern=[[-1, 128]], channel_multiplier=1), reads=["identf"], writes=["identf"])
        P.op("pool", lambda e: e.memset(mhalf[:], -0.5), writes=["mhalf"])
        kA = [("stA", c) for c in range(KC)]
        kB = [("stB", c) for c in range(KC)]
        kH = [("stH", c) for c in range(KC)]
        P.op("pool", lambda e: e.memset(stA[:], 0.0), writes=kA)
        P.op("pool", lambda e: e.memset(stB[:], 0.0), writes=kB)
        P.op("pool", lambda e: e.memset(stH[:], 0.0), writes=kH)
        cp(identb[:], identf[:], ["identf"], ["identb"])
        P.op("pool", lambda e: e.memset(dv[:, D_ZERO:D_ZERO + 1], 0.0), writes=["dvz"])
        vec_srcs = [(conv_a_w_d, CAW, 24), (conv_b_w_d, CBW, 32), (conv_b_b_d, CBB, 8), (b_rg_a_d, BRA, 8),
                    (b_rg_i_d, BRI, 8), (lru_d, LAM, 8), (b_gate_d, BG, 16)]
        for i, (src, r0, n) in enumerate(vec_srcs):
            dma(vecs[r0:r0 + n, :], src.rearrange("k (j p) -> (k j) p", p=128), [], [("vecs", i)], "cst")
        dma(gbc[:], norm_g_d.partition_broadcast(128), [], ["gbc"], "cstg")
        dma(fgbc[:], fg_d.partition_broadcast(128), [], ["fgbc"], "cstf")
        pst, pk = psum()
        tr_(pst[:, 0:NV], vecs[:], identf[0:NV, 0:NV], [("vecs", i) for i in range(7)] + ["identf"], [pk])
        cp(vT[:], pst[:, 0:NV], [pk], ["vT"])
        ts(dv[:, D_CAWH:D_CAWH + 24], vT[:, CAW:CAW + 24], 0.5, None, ALU.mult, None, ["vT"], ["dv"])
        ts(dv[:, D_HBRA:D_HBRA + 16], vT[:, BRA:BRA + 16], 0.5, None, ALU.mult, None, ["vT"], ["dv"])
        ts(dv[:, D_HBG:D_HBG + 16], vT[:, BG:BG + 16], 0.5, None, ALU.mult, None, ["vT"], ["dv"])
        act(dv[:, D_TMP:D_TMP + 8], vT[:, LAM:LAM + 8], AF.Exp, ["vT"], ["dv"], scale=-1.0)
        act(dv[:, D_TMP + 8:D_TMP + 16], dv[:, D_TMP:D_TMP + 8], AF.Ln, ["dv"], ["dv"], bias=1.0, scale=1.0)
        ts(dv[:, D_C2:D_C2 + 8], dv[:, D_TMP + 8:D_TMP + 16], -4.0, None, ALU.mult, None, ["dv"], ["dv"])
        ts(dv[:, D_C4:D_C4 + 8], dv[:, D_TMP + 8:D_TMP + 16], -8.0, None, ALU.mult, None, ["dv"], ["dv"])

        xtc = {"n": 0}

        def rstd_act(sqt, n, sqk):
            act(sqt[0:n, 1:2], sqt[0:n, 0:1], AF.Sqrt, [sqk], [sqk], bias=EPS, scale=1.0 / D)

        def rstd_dve(sqt, n, sqk):
            P.op("dve", lambda e: e.reciprocal(out=sqt[0:n, 2:3], in_=sqt[0:n, 1:2]), [sqk], [sqk])

        def item_load(it):
            bi = xtc["n"] % NXT
            xtc["n"] += 1
            it["xb"] = bi
            n = it["n"]
            dma(xt[bi][0:n, :], it["src"], [], [("xt", bi)], f"xl{bi}")

        def new_sq(it):
            sqi = cnt["sq"] % 4
            cnt["sq"] += 1
            it["sq"] = (sq[sqi], ("sq", sqi))

        def in_A_act(it):
            n, bi = it["n"], it["xb"]
            new_sq(it)
            sqt, sqk = it["sq"]
            act(junk[0:n, :], xt[bi][0:n, :], AF.Square, [("xt", bi)], [sqk], accum_out=sqt[0:n, 0:1])
            rstd_act(sqt, n, sqk)

        def in_A_pool(it):
            sqt, sqk = it["sq"]
            rstd_dve(sqt, it["n"], sqk)

        def in_B1_dve(it):
            n, bi = it["n"], it["xb"]
            sqt, sqk = it["sq"]
            jb = cnt["xs"] % 3
            cnt["xs"] += 1
            xsb, xsk = xs[jb], ("xs", jb)
            stt(xsb[0:n, :], xt[bi][0:n, :], sqt[0:n, 2:3], gbc[0:n, :], ALU.mult, ALU.mult, [("xt", bi), sqk, "gbc"], [xsk])
            it["xs"] = (xsb, xsk)

        def in_B1_pool(it):
            pass

        def in_B2(it):
            n = it["n"]
            xsb, xsk = it["xs"]
            pt, pk = psum()
            ptb = pt[:].bitcast(BF16)
            for kc in range(KC):
                tr_(ptb[:, kc * 128:kc * 128 + n], xsb[0:n, kc * 128:(kc + 1) * 128], identb[0:n, 0:n], [xsk, "identb"], [pk])
            it["pt"] = (ptb, pk)

        def in_C(it):
            n, g = it["n"], it["g"]
            ptb, pk = it["pt"]
            c0 = it.get("c0", g * 128)
            xk = it.get("xk", ("xn", (g * 128) // SW))
            act(xnT[:, :, c0:c0 + n], ptb.rearrange("p (k t) -> p k t", k=KC)[:, :, 0:n], AF.Copy, [pk], [xk])

        def p3_A_pe(it):
            g = it["g"]
            s_ = (g * 128) // SW
            it["po"] = []
            for half in range(2):
                po, pok = psum()
                for kc in range(KC):
                    mm(po[:], mT[:, kc, g * 128:(g + 1) * 128], wo[:, kc, half * SW:(half + 1) * SW], kc == 0, kc == KC - 1,
                       [("m", kc, s_), "wo"], [pok])
                it["po"].append((po, pok))

        def p3_A_dve(it, half):
            bi = it["xb"]
            xtb, xtk = xt[bi], ("xt", bi)
            if True:
                po, pok = it["po"][half]
                stt(xtb[:, half * SW:(half + 1) * SW], po[:], 0.5, xtb[:, half * SW:(half + 1) * SW], ALU.mult, ALU.add,
                    [pok, xtk], [xtk])

        def p3_B_act(it):
            bi = it["xb"]
            new_sq(it)
            sqt, sqk = it["sq"]
            act(junk[:], xt[bi][:], AF.Square, [("xt", bi)], [sqk], accum_out=sqt[:, 0:1])
            rstd_act(sqt, 128, sqk)

        def p3_B_pool(it):
            sqt, sqk = it["sq"]
            rstd_dve(sqt, 128, sqk)

        def p3_C(it):
            bi = it["xb"]
            xtb, xtk = xt[bi], ("xt", bi)
            sqt, sqk = it["sq"]
            stt(xtb[:], xtb[:], sqt[:, 2:3], fgbc[:], ALU.mult, ALU.mult, [xtk, sqk, "fgbc"], [xtk])
            dma(it["dst"], xtb[:], [xtk], [], f"st{bi}")

        def prefetch_boundary(p3, nin, k=4):
            items = []
            for i in range(max(len(p3), len(nin))):
                if i < len(p3):
                    items.append(p3[i])
                if i < len(nin):
                    items.append(nin[i])
            for it in items[:min(k, NXT - 1)]:
                item_load(it)
                it["loaded"] = True

        def run_boundary(p3, nin, tail_hook=None):
            n = max(len(p3), len(nin))
            loads = []
            for i in range(n):
                if i < len(p3):
                    loads.append(p3[i])
                if i < len(nin):
                    loads.append(nin[i])
            nl = [0]
            for it in loads:
                it["fin"] = False

            def load_more(maxahead):
                while nl[0] < len(loads) and nl[0] < maxahead:
                    k = nl[0]
                    if loads[k].get("loaded"):
                        nl[0] += 1
                        continue
                    if k >= NXT and not loads[k - NXT]["fin"]:
                        break
                    item_load(loads[k])
                    nl[0] += 1
            per = (1 if p3 else 0) + (1 if nin else 0)
            P_ = lambda k: p3[k] if 0 <= k < len(p3) else None
            I_ = lambda k: nin[k] if 0 <= k < len(nin) else None
            for t in range(n + 3):
                if t == n and tail_hook is not None:
                    tail_hook()
                load_more(per * (t + 2))
                if I_(t - 2):
                    in_B2(I_(t - 2))
                if P_(t):
                    p3_A_pe(P_(t))
                if P_(t - 1):
                    p3_B_act(P_(t - 1))
                if I_(t):
                    in_A_act(I_(t))
                if I_(t - 2):
                    in_C(I_(t - 2))
                if I_(t - 1):
                    in_B1_dve(I_(t - 1))
                    in_B1_pool(I_(t - 1))
                    I_(t - 1)["fin"] = True
                if P_(t - 1):
                    p3_B_pool(P_(t - 1))
                load_more(per * (t + 2) + 1)
                if P_(t - 1):
                    p3_C(P_(t - 1))
                    P_(t - 1)["fin"] = True
                if P_(t):
                    p3_A_dve(P_(t), 0)
                if I_(t):
                    in_A_pool(I_(t))
                if P_(t):
                    p3_A_dve(P_(t), 1)

        def in_items(b, n):
            return [dict(kind="IN", g=g, n=128, src=x_d[b, n * T + g * 128:n * T + (g + 1) * 128, :]) for g in range(NG)]

        def p3_items(b, n):
            return [dict(kind="P3", g=g, n=128, src=x_d[b, n * T + g * 128:n * T + (g + 1) * 128, :],
                         dst=out_d[b, n * T + g * 128:n * T + (g + 1) * 128, :]) for g in range(NG)]

        stg = [yaT[:].bitcast(F32), ybT[:].bitcast(F32), xnT[:].bitcast(F32)[:, :, 0:T // 2]]
        stg_keys = [ya_keys, yb_keys, [("xn", s) for s in range(NS)] + [("xn", "m")]]
        NSTG = 3
        mflat = mT[:].rearrange("p k t -> p (k t)")
        wb = [mflat[:, i * 4096:(i + 1) * 4096].rearrange("p (j k c) -> p j k c", j=4, k=KC) for i in range(2)]
        wb_keys = [[("m", j, s) for j in range(4 * i, 4 * i + 4) for s in range(NS)] for i in range(2)]
        def prologue_weights():
            rounds = []
            for g in (4, 5, 0, 1, 2, 3, 6, 7):
                for jh in range(2):
                    c0 = g * D + jh * 512
                    rounds.append(dict(src=w_in_d[:, c0:c0 + 512].rearrange("(kc p) n -> p kc n", p=128), blk0=g * 8 + jh * 4))
            for jh in range(2):
                rounds.append(dict(src=w_proj_a_d[:, jh * 512:(jh + 1) * 512].rearrange("(kc p) n -> p kc n", p=128), blk0=64 + jh * 4))
            for jh in range(2):
                rounds.append(dict(src=w_proj_b_d[:, jh * 512:(jh + 1) * 512].rearrange("(kc p) n -> p kc n", p=128), blk0=72 + jh * 4))
            for jh in range(2):
                rounds.append(dict(src=w_out_d[:, jh * 512:(jh + 1) * 512].rearrange("(kc p) n -> p kc n", p=128),
                                   sb=wo[:, :, jh * 512:(jh + 1) * 512], sbk=["wo"]))
            for gi, wsrc in enumerate((w_rg_a_d, w_rg_i_d)):
                rounds.append(dict(src=wsrc.rearrange("h (kc p) n -> p (h kc) n", p=128), sb=wrg[:, gi, :, :], sbk=["wrg"], ncol=256))

            def do_load(r):
                rd = rounds[r]
                i = r % NSTG
                dst = stg[i] if "ncol" not in rd else stg[i][:, :, 0:rd["ncol"]]
                dma(dst, rd["src"], [], stg_keys[i], f"pl{i}")

            nwb = [0]

            def do_cast_store(r):
                rd = rounds[r]
                i = r % NSTG
                if "blk0" in rd:
                    wi = nwb[0] % 2
                    nwb[0] += 1
                    o_ap = wb[wi].rearrange("p j k c -> p k j c")
                    i_ap = stg[i].rearrange("p k (j c) -> p k j c", j=4)
                    cp(o_ap, i_ap, stg_keys[i], wb_keys[wi])
                    b0 = rd["blk0"]
                    dma(wsc[b0:b0 + 4].rearrange("b p n -> p b n"), wb[wi].rearrange("p j k c -> p j (k c)"),
                        wb_keys[wi], [("wsc", b0 + q) for q in range(4)], f"pst{wi}")
                else:
                    src = stg[i] if "ncol" not in rd else stg[i][:, :, 0:rd["ncol"]]
                    act(rd["sb"], src, AF.Copy, stg_keys[i], rd["sbk"])

            for r in range(min(NSTG - 1, len(rounds))):
                do_load(r)
            for r in range(len(rounds)):
                if r + NSTG - 1 < len(rounds):
                    do_load(r + NSTG - 1)
                do_cast_store(r)

        tiles = [("seq", b, n) for b in range(NSEQ) for n in range(SEQ // T)]
        seq_blocks = []
        for t in tiles:
            for h in range(4):
                for g in (4, 5):
                    for c in (2 * h, 2 * h + 1):
                        seq_blocks.append(g * 8 + c)
                for c in (2 * h, 2 * h + 1):
                    for g in (0, 1, 2, 3):
                        seq_blocks.append(g * 8 + c)
            if t[0] != "meta":
                for j in range(KC):
                    seq_blocks += [48 + j, 56 + j, 64 + j, 72 + j]
        bs = {"issued": 0, "used": 0, "done": 0}

        def issue_loads():
            upto = min(bs["done"] + NSLOT, len(seq_blocks))
            while bs["issued"] < upto:
                k = bs["issued"]
                blk = seq_blocks[k]
                si = k % NSLOT
                dma(slots[si][:].rearrange("p k c -> p (k c)"), wsc[blk], [("wsc", blk)], [("slot", si)], f"wl{si}")
                bs["issued"] += 1

        def next_block(blk):
            k = bs["used"]
            assert seq_blocks[k] == blk, (k, seq_blocks[k], blk)
            assert k < bs["issued"], (k, bs["issued"])
            bs["used"] += 1
            si = k % NSLOT
            return slots[si], ("slot", si)

        def blocks_done(n):
            bs["done"] += n
            assert bs["done"] == bs["used"]
            issue_loads()

        def proj_group(slot, skey, s, c0, w):
            pt, pk = psum()
            for kc in range(KC):
                mm(pt[:, 0:w], slot[:, kc, :], xnT[:, kc, c0:c0 + w], kc == 0, kc == KC - 1, [skey, ("xn", s)], [pk])
            return pt, pk

        unit = [0]
        zcol = dv[:, D_ZERO:D_ZERO + 1]

        def interleave(*lists):
            out = []
            n = max(len(l) for l in lists)
            for i in range(n):
                for l in lists:
                    if i < len(l):
                        out.append(l[i])
            for f in out:
                f()

        def pre_project(subtiles):
            cs = (0, 1)
            sl_xb = [next_block(4 * 8 + c) for c in cs]
            sl_zb = [next_block(5 * 8 + c) for c in cs]
            s, c0, w = subtiles[0]
            pz = [proj_group(sl_zb[ci][0], sl_zb[ci][1], s, c0, w) for ci in range(2)]
            pt = [proj_group(sl_xb[ci][0], sl_xb[ci][1], s, c0, w) for ci in range(2)]
            return dict(sl_xb=sl_xb, sl_zb=sl_zb, pz=pz, pt=pt, s=s)

        def phase1(subtiles, meta_sub=None, pre=None):
            for h in range(4):
                cs = (2 * h, 2 * h + 1)
                if h == 0 and pre:
                    sl_xb, sl_zb = pre["sl_xb"], pre["sl_zb"]
                else:
                    sl_xb = [next_block(4 * 8 + c) for c in cs]
                    sl_zb = [next_block(5 * 8 + c) for c in cs]
                bst = []

                def b_s123(s, c0, w, is_meta):
                    if is_meta:
                        u = 2
                    else:
                        u = unit[0] % 2
                        unit[0] += 1
                    st = dict(u=u, s=s, c0=c0, w=w, meta=is_meta)
                    xcbt, xcbk = xcb[u], ("xcb", u)
                    for ci, c in enumerate(cs):
                        if is_meta:
                            break
                        bi = u * 2 + ci
                        if h == 0 and pre and pre["s"] == s and not is_meta:
                            pz, pzk = pre["pz"][ci]
                        else:
                            pz, pzk = proj_group(sl_zb[ci][0], sl_zb[ci][1], s, c0, w)
                        act(scr["tz"][bi][:, 0:w], pz[:, 0:w], AF.Tanh, [pzk], [("tz", bi)], scale=0.5)
                        stt(scr["tz"][bi][:, 0:w], scr["tz"][bi][:, 0:w], 1.0, pz[:, 0:w], ALU.add, ALU.mult,
                            [("tz", bi), pzk], [("tz", bi)])
                    for ci, c in enumerate(cs):
                        bi = u * 2 + ci
                        xb_t, xb_k = scr["xbuf"][bi], ("xbuf", bi)
                        xc_t, xc_k = scr["xc"][bi], ("xc", bi)
                        if h == 0 and pre and pre["s"] == s and not is_meta:
                            pt, pk = pre["pt"][ci]
                        else:
                            pt, pk = proj_group(sl_xb[ci][0], sl_xb[ci][1], s, c0, w)
                        act(xb_t[:, 0:3], stB[:, c, :], AF.Copy, [("stB", c)], [xb_k])
                        act(xb_t[:, 3:3 + w], pt[:, 0:w], AF.Copy, [pk], [xb_k])
                        act(stB[:, c, :], xb_t[:, w:w + 3], AF.Copy, [xb_k], [("stB", c)])
                        ts(xc_t[:, 0:w], xb_t[:, 0:w], vT[:, CBW + c:CBW + c + 1], vT[:, CBB + c:CBB + c + 1],
                           ALU.mult, ALU.add, [xb_k, "vT"], [xc_k])
                        for k in range(1, 4):
                            stt(xc_t[:, 0:w], xb_t[:, k:k + w], vT[:, CBW + 8 * k + c:CBW + 8 * k + c + 1], xc_t[:, 0:w],
                                ALU.mult, ALU.add, [xb_k, xc_k, "vT"], [xc_k])
                        cp(xcbt[:, ci, 0:w], xc_t[:, 0:w], [xc_k], [xcbk])
                    return st

                def b_gates_pe(st):
                    u, s, c0, w = st["u"], st["s"], st["c0"], st["w"]
                    xcbt, xcbk = xcb[u], ("xcb", u)
                    gp = []
                    for ci, c in enumerate(cs):
                        pr, prk = psum()
                        for kc in range(2):
                            mm(pr[:, 0:w], wrg[:, 0, h * 2 + kc, ci * 128:(ci + 1) * 128], xcbt[:, kc, 0:w], kc == 0, kc == 1,
                               ["wrg", xcbk], [prk])
                        pi, pik = psum()
                        for kc in range(2):
                            mm(pi[:, 0:w], wrg[:, 1, h * 2 + kc, ci * 128:(ci + 1) * 128], xcbt[:, kc, 0:w], kc == 0, kc == 1,
                               ["wrg", xcbk], [pik])
                        gp.append((pr, prk, pi, pik))
                    st["gp"] = gp

                def b_gates_act(st, ci):
                    u, s, c0, w = st["u"], st["s"], st["c0"], st["w"]
                    c = cs[ci]
                    bi = u * 2 + ci
                    pr, prk, pi, pik = st["gp"][ci]
                    act(scr["tr"][ci][:, 0:w], pr[:, 0:w], AF.Tanh, [prk, "dv"], [("tr", ci)],
                        bias=dv[:, D_HBRA + c:D_HBRA + c + 1], scale=0.5)
                    act(scr["ti"][ci][:, 0:w], pi[:, 0:w], AF.Tanh, [pik, "dv"], [("ti", ci)],
                        bias=dv[:, D_HBRI + c:D_HBRI + c + 1], scale=0.5)
                    act(scr["aa"][ci][:, 0:w], scr["tr"][ci][:, 0:w], AF.Exp, [("tr", ci), "dv"], [("aa", ci)],
                        bias=dv[:, D_C2 + c:D_C2 + c + 1], scale=dv[:, D_C2 + c:D_C2 + c + 1])
                    act(scr["mm"][bi][:, 0:w], scr["tr"][ci][:, 0:w], AF.Exp, [("tr", ci), "dv"], [("mm", bi)],
                        bias=dv[:, D_C4 + c:D_C4 + c + 1], scale=dv[:, D_C4 + c:D_C4 + c + 1])
                    act(scr["mm"][bi][:, 0:w], scr["mm"][bi][:, 0:w], AF.Relu, [("mm", bi)], [("mm", bi)],
                        bias=1.0 / 16, scale=-1.0 / 16)

                def b_sqrt_act(st):
                    u, w = st["u"], st["w"]
                    for ci, c in enumerate(cs):
                        bi = u * 2 + ci
                        act(scr["mm"][bi][:, 0:w], scr["mm"][bi][:, 0:w], AF.Sqrt, [("mm", bi)], [("mm", bi)])

                def b_tail_ops(st):
                    u, s, c0, w = st["u"], st["s"], st["c0"], st["w"]
                    is_meta = st["meta"]
                    ops = []
                    for ci, c in enumerate(cs):
                        bi = u * 2 + ci
                        hh, hk = scr["xc"][bi], ("xc", bi)
                        ti_t, ti_k = scr["ti"][ci], ("ti", ci)
                        aa_t, aa_k = scr["aa"][ci], ("aa", ci)
                        ops.append(lambda bi=bi, ti_t=ti_t, ti_k=ti_k: stt(ti_t[:, 0:w], ti_t[:, 0:w], 1.0, scr["xc"][bi][:, 0:w],
                                                                           ALU.add, ALU.mult, [ti_k, ("xc", bi)], [ti_k]))
                        if is_meta:
                            ops.append(lambda bi=bi: P.op("dve", lambda e, t=scr["mm"][bi]: e.memset(t[:, 0:1], 0.25), [], [("mm", bi)]))
                        ops.append(lambda bi=bi, ti_t=ti_t, ti_k=ti_k: tt(ti_t[:, 0:w], scr["mm"][bi][:, 0:w], ti_t[:, 0:w], ALU.mult,
                                                                          [("mm", bi), ti_k], [ti_k]))
                        ops.append(lambda c=c, hh=hh, hk=hk, ti_t=ti_t, ti_k=ti_k, aa_t=aa_t, aa_k=aa_k: P.op(
                            "dve", lambda e, o=hh[:, 0:w], a=aa_t[:, 0:w], uu=ti_t[:, 0:w], ini=stH[:, c, :]:
                            e.tensor_tensor_scan(out=o, data0=a, data1=uu, initial=ini, op0=ALU.mult, op1=ALU.add),
                            [aa_k, ti_k, ("stH", c)], [hk]))
                        ops.append(lambda c=c, hh=hh, hk=hk: cp(stH[:, c, :], hh[:, w - 1:w], [hk], [("stH", c)]))
                        if not is_meta:
                            ops.append(lambda bi=bi, c=c, hh=hh, hk=hk: tt(ybT[:, c, c0:c0 + w], hh[:, 0:w], scr["tz"][bi][:, 0:w], ALU.mult,
                                                                           [hk, ("tz", bi)], [("yb", c, s)], eng="pool"))
                    return ops

                def a_pe(c, sl, s, c0, w):
                    ai = unit[0] % 2
                    unit[0] += 1
                    ph, phk = proj_group(sl[2][0], sl[2][1], s, c0, w)
                    pz, pzk = proj_group(sl[3][0], sl[3][1], s, c0, w)
                    pc, pck = proj_group(sl[1][0], sl[1][1], s, c0, w)
                    pb, pbk = proj_group(sl[0][0], sl[0][1], s, c0, w)
                    return dict(ai=ai, c=c, s=s, c0=c0, w=w, pb=(pb, pbk), pc=(pc, pck), ph=(ph, phk), pz=(pz, pzk))

                def a_act(au):
                    ai, w = au["ai"], au["w"]
                    act(scr["cv"][ai][:, 0:w], au["ph"][0][:, 0:w], AF.Copy, [au["ph"][1]], [("cv", ai)])
                    act(scr["tza"][ai][:, 0:w], au["pz"][0][:, 0:w], AF.Tanh, [au["pz"][1]], [("tza", ai)], scale=0.5)

                def a_dve_ops(au, is_meta=False):
                    ai, c, s, c0, w = au["ai"], au["c"], au["s"], au["c0"], au["w"]
                    pb, pbk = au["pb"]
                    pc, pck = au["pc"]
                    pz, pzk = au["pz"]
                    tz_t, tz_k = scr["tza"][ai], ("tza", ai)
                    ch_t, ch_k = scr["ch"][ai], ("ch", ai)
                    cv_t, cv_k = scr["cv"][ai], ("cv", ai)
                    ops = []
                    ops.append(lambda: cp(ch_t[:, 0:2], stA[:, c, :], [("stA", c)], [ch_k]))
                    ops.append(lambda: tt(ch_t[:, 2:2 + w], pc[:, 0:w], cv_t[:, 0:w], ALU.mult, [pck, cv_k], [ch_k]))
                    ops.append(lambda: stt(tz_t[:, 0:w], tz_t[:, 0:w], 1.0, pz[:, 0:w], ALU.add, ALU.mult, [tz_k, pzk], [tz_k]))
                    ops.append(lambda: cp(stA[:, c, :], ch_t[:, w:w + 2], [ch_k], [("stA", c)]))
                    if is_meta:
                        return [ops[0], ops[1], ops[3]]
                    ops.append(lambda: tt(tz_t[:, 0:w], tz_t[:, 0:w], pb[:, 0:w], ALU.mult, [tz_k, pbk], [tz_k]))

                    def pool_part():
                        at_t, at_k = scr["atmp"][0], ("atmp", 0)
                        ts(cv_t[:, 0:w], ch_t[:, 0:w], dv[:, D_CAWH + c:D_CAWH + c + 1], zcol, ALU.mult, ALU.add,
                           [ch_k, "dv", "dvz"], [cv_k], eng="pool")
                        for k in range(1, 3):
                            ts(at_t[:, 0:w], ch_t[:, k:k + w], dv[:, D_CAWH + 8 * k + c:D_CAWH + 8 * k + c + 1], zcol, ALU.mult, ALU.add,
                               [ch_k, "dv", "dvz"], [at_k], eng="pool")
                            tt(cv_t[:, 0:w], cv_t[:, 0:w], at_t[:, 0:w], ALU.add, [cv_k, at_k], [cv_k], eng="pool")
                        tt(yaT[:, c, c0:c0 + w], cv_t[:, 0:w], tz_t[:, 0:w], ALU.mult, [cv_k, tz_k], [("ya", c, s)], eng="pool")
                    ops.append(pool_part)
                    return ops

                if meta_sub is not None:
                    bst.append(b_s123(meta_sub[0], meta_sub[1], meta_sub[2], True))
                    for c in cs:
                        act(mtB[:, c, :], stB[:, c, :], AF.Copy, [("stB", c)], [("mtB", c)])
                for (s, c0, w) in subtiles:
                    bst.append(b_s123(s, c0, w, False))
                blocks_done(4)

                a_units = [(c, si) for c in cs for si in range(len(subtiles))]
                sl = None
                for idx, (c, si) in enumerate(a_units):
                    if si == 0:
                        sl = [next_block(g * 8 + c) for g in range(4)]
                        if meta_sub is not None:
                            mau = a_pe(c, sl, meta_sub[0], meta_sub[1], meta_sub[2])
                            a_act(mau)
                            interleave(a_dve_ops(mau, True))
                            cp(mtA[:, c, :], stA[:, c, :], [("stA", c)], [("mtA", c)])
                    s, c0, w = subtiles[si]
                    bq = idx - (len(a_units) - len(bst))
                    has_b = 0 <= bq < len(bst)
                    if has_b:
                        b_gates_pe(bst[bq])
                    au = a_pe(c, sl, s, c0, w)
                    if has_b:
                        b_gates_act(bst[bq], 0)
                        a_act(au)
                        b_gates_act(bst[bq], 1)
                        b_sqrt_act(bst[bq])
                        interleave(a_dve_ops(au))
                        interleave(b_tail_ops(bst[bq]))
                        if bst[bq]["meta"]:
                            for c_ in cs:
                                cp(mtH[:, c_, :], stH[:, c_, :], [("stH", c_)], [("mtH", c_)])
                    else:
                        a_act(au)
                        interleave(a_dve_ops(au))
                    if si == len(subtiles) - 1:
                        blocks_done(4)

        def phase2(subtiles):
            for j in range(KC):
                sga = next_block(48 + j)
                sgb = next_block(56 + j)
                spa = next_block(64 + j)
                spb = next_block(72 + j)
                for (s, c0, w) in subtiles:
                    bi = unit[0] % 2
                    unit[0] += 1
                    pga, pgak = proj_group(sga[0], sga[1], s, c0, w)
                    pgb, pgbk = proj_group(sgb[0], sgb[1], s, c0, w)
                    pya, pyak = psum()
                    for kc in range(KC):
                        mm(pya[:, 0:w], spa[0][:, kc, :], yaT[:, kc, c0:c0 + w], kc == 0, kc == KC - 1,
                           [spa[1], ("ya", kc, s)], [pyak])
                    pyb, pybk = psum()
                    for kc in range(KC):
                        mm(pyb[:, 0:w], spb[0][:, kc, :], ybT[:, kc, c0:c0 + w], kc == 0, kc == KC - 1,
                           [spb[1], ("yb", kc, s)], [pybk])
                    tga, tgak = scr["tr"][bi], ("tr", bi)
                    tgb, tgbk = scr["ti"][bi], ("ti", bi)
                    act(tga[:, 0:w], pga[:, 0:w], AF.Tanh, [pgak, "dv"], [tgak], bias=dv[:, D_HBG + j:D_HBG + j + 1], scale=0.5)
                    act(tgb[:, 0:w], pgb[:, 0:w], AF.Tanh, [pgbk, "dv"], [tgbk], bias=dv[:, D_HBG + 8 + j:D_HBG + 8 + j + 1], scale=0.5)
                    stt(tga[:, 0:w], tga[:, 0:w], 1.0, pya[:, 0:w], ALU.add, ALU.mult, [tgak, pyak], [tgak])
                    stt(tgb[:, 0:w], tgb[:, 0:w], 1.0, pyb[:, 0:w], ALU.add, ALU.mult, [tgbk, pybk], [tgbk])
                    tt(mT[:, j, c0:c0 + w], tga[:, 0:w], tgb[:, 0:w], ALU.add, [tgak, tgbk], [("m", j, s)])
                blocks_done(4)

        kmA = [("mtA", c) for c in range(KC)]
        kmB = [("mtB", c) for c in range(KC)]
        kmH = [("mtH", c) for c in range(KC)]
        prologue_weights()
        run_boundary([], [dict(kind="IN", g=0, n=NMETA, src=meta_d, c0=T, xk=("xn", "m"))])
        issue_loads()
        subtiles = [(s, s * SW, SW) for s in range(NS)]
        seq_tiles = tiles
        run_boundary([], in_items(seq_tiles[0][1], seq_tiles[0][2]))
        pre_next = {}
        for ti, (_, b, n) in enumerate(seq_tiles):
            if n == 0 and ti > 0:
                cp(stA[:], mtA[:], kmA, kA)
                cp(stB[:], mtB[:], kmB, kB)
                cp(stH[:], mtH[:], kmH, kH)
            phase1(subtiles, meta_sub=(("m", T, NMETA) if ti == 0 else None), pre=(pre_next if ti > 0 else None))
            p3 = p3_items(b, n)
            nin = in_items(seq_tiles[ti + 1][1], seq_tiles[ti + 1][2]) if ti + 1 < len(seq_tiles) else []
            prefetch_boundary(p3, nin)
            phase2(subtiles)
            pre_next = {}
            run_boundary(p3, nin, tail_hook=((lambda d=pre_next: d.update(pre_project(subtiles))) if nin else None))
        P.op("sp", None, writes=[("xt", i) for i in range(NXT)])

        P.finalize()
        csem = {e: es.enter_context(nc.semaphore("c_" + e)) for e in Prog.ENGS}
        dsem = {k: es.enter_context(nc.semaphore("d_" + k)) for k in sorted(P.dma_cnt.keys())}
        with nc.Block() as block:
            @block.tensor
            def _(e):
                P.emit("pe", e, csem, dsem)

            @block.scalar
            def _(e):
                P.emit("act", e, csem, dsem)

            @block.vector
            def _(e):
                P.emit("dve", e, csem, dsem)

            @block.gpsimd
            def _(e):
                P.emit("pool", e, csem, dsem)

            @block.sync
            def _(e):
                P.emit("sp", e, csem, dsem)
    return nc


def _weights_map(inputs):
    f = lambda a: np.ascontiguousarray(np.asarray(a, dtype=np.float32))
    return {
        "meta": f(inputs["meta"]),
        "norm_g": f(inputs["norm_g"]).reshape(1, D),
        "w_in": f(inputs["w_in"]).reshape(D, 8 * D),
        "b_gate": f(inputs["b_gate"]).reshape(2, D),
        "conv_a_w": f(inputs["conv_a_w"]).reshape(3, D),
        "w_proj_a": f(inputs["w_proj_a"]).reshape(D, D),
        "conv_b_w": f(inputs["conv_b_w"]).reshape(4, D),
        "conv_b_b": f(inputs["conv_b_b"]).reshape(1, D),
        "w_rg_a": f(inputs["w_rg_a"]).reshape(4, 256, 256),
        "b_rg_a": f(inputs["b_rg_a"]).reshape(1, D),
        "w_rg_i": f(inputs["w_rg_i"]).reshape(4, 256, 256),
        "b_rg_i": f(inputs["b_rg_i"]).reshape(1, D),
        "lru_param": f(inputs["lru_param"]).reshape(1, D),
        "w_proj_b": f(inputs["w_proj_b"]).reshape(D, D),
        "w_out": f(inputs["w_out"]).reshape(D, D),
        "final_norm_g": f(inputs["final_norm_g"]).reshape(1, D),
    }


def run(inputs, n_cores, T=1024, trace=False):
    x = np.asarray(inputs["x"], dtype=np.float32)
    B, SEQ, _ = x.shape
    assert B % n_cores == 0
    NSEQ = B // n_cores
    nc = build_nc(NSEQ, SEQ, T)
    wm = _weights_map(inputs)
    in_maps = []
    for c in range(n_cores):
        m = dict(wm)
        m["x"] = np.ascontiguousarray(x[c * NSEQ:(c + 1) * NSEQ])
        in_maps.append(m)
    res = run_bass_kernel_spmd(nc, in_maps, core_ids=list(range(n_cores)), trace=trace)
    out = np.concatenate([np.asarray(r["out"]) for r in res.results], axis=0)
    return out.astype(np.float32), res


def kernel(**inputs):
    out, _ = run(inputs, N_CORES)
    return out
```
